# Optimizing a Trainium2 kernel written in Bass

```python
import jax, jax.numpy as jnp
from jax import lax
import numpy as np

D_MODEL = 4096
BATCH = 8
SEQ = 2048
DEPTH = 2

MIX_WIDTH = D_MODEL
W_A = MIX_WIDTH // 4
W_B = MIX_WIDTH // 4
W_C = MIX_WIDTH // 4
W_D = MIX_WIDTH // 4
POOL_WINDOWS = (2, 4, 8, 16)
N_POOL_GROUPS = len(POOL_WINDOWS)
POOL_GROUP_DIM = W_A // N_POOL_GROUPS
CHUNK = 128
SGU_HEAD_DIM = 128
SGU_HEADS = W_B // SGU_HEAD_DIM
SHORT_CONV_K = 3
CONF_CONV_K = 31
N_BRANCH = 4
FFN_HIDDEN = 11008
FFN_CONV_K = 3
N_IN = W_A + 2 * W_B + 3 * W_C + 2 * W_D + N_BRANCH * D_MODEL
IN_SPLITS = (W_A, W_A + 2 * W_B, W_A + 2 * W_B + 3 * W_C, W_A + 2 * W_B + 3 * W_C + 2 * W_D)
BRANCH_ROWS = ((0, W_A), (W_A, W_A + W_B), (W_A + W_B, W_A + W_B + W_C), (W_A + W_B + W_C, MIX_WIDTH))
RMS_EPS = 1e-6
LN_EPS = 1e-5

kernel_name = "hybrid_pool_sgu_shortconv_conformer_convffn"


def rms_norm(x, g):
    xf = x.astype(jnp.float32)
    y = xf * lax.rsqrt(jnp.mean(xf * xf, axis=-1, keepdims=True) + RMS_EPS)
    return (y * g.astype(jnp.float32)).astype(x.dtype)


def layer_norm(x, g, b):
    xf = x.astype(jnp.float32)
    mu = jnp.mean(xf, axis=-1, keepdims=True)
    xc = xf - mu
    var = jnp.mean(xc * xc, axis=-1, keepdims=True)
    y = xc * lax.rsqrt(var + LN_EPS)
    return (y * g.astype(jnp.float32) + b.astype(jnp.float32)).astype(x.dtype)


def causal_dwconv(x, w):
    k, ch = w.shape
    return lax.conv_general_dilated(
        x, w[:, None, :].astype(x.dtype), window_strides=(1,), padding=[(k - 1, 0)],
        dimension_numbers=('NWC', 'WIO', 'NWC'), feature_group_count=ch)


def pool_mixer(a, pool_w, pool_scale):
    bsz, s, _ = a.shape
    af = a.astype(jnp.float32).reshape(bsz, s, N_POOL_GROUPS, POOL_GROUP_DIM)
    csum = jnp.cumsum(af, axis=1)
    t = jnp.arange(s)
    outs = []
    for g, w in enumerate(POOL_WINDOWS):
        cg = csum[:, :, g]
        lag = jnp.pad(cg, ((0, 0), (w, 0), (0, 0)))[:, :s]
        cnt = jnp.minimum(t + 1, w).astype(jnp.float32)[None, :, None]
        outs.append((cg - lag) / cnt - af[:, :, g])
    pooled = jnp.stack(outs, axis=2).astype(a.dtype)
    y = jnp.einsum('bsgc,gcd->bsgd', pooled, pool_w).reshape(bsz, s, W_A)
    return y * pool_scale


def sgu_mixer(z, ln_g, ln_b, w_s, b_s):
    z = jax.nn.gelu(z, approximate=False)
    u, v = jnp.split(z, 2, axis=-1)
    v = layer_norm(v, ln_g, ln_b)
    bsz, s, _ = v.shape
    v = v.reshape(bsz, s // CHUNK, CHUNK, SGU_HEADS, SGU_HEAD_DIM)
    mask = jnp.tril(jnp.ones((CHUNK, CHUNK), dtype=bool))
    w = jnp.where(mask[None], w_s, 0)
    sv = jnp.einsum('hts,bnshc->bnthc', w, v) + jnp.transpose(b_s)[None, None, :, :, None]
    return u * sv.reshape(bsz, s, W_B)


def short_conv_mixer(z, conv_w):
    bg, cg, hx = jnp.split(z, 3, axis=-1)
    return bg * causal_dwconv(cg * hx, conv_w)


def conformer_conv_mixer(z, dw_w, dw_b, ln_g, ln_b):
    a, g = jnp.split(z, 2, axis=-1)
    y = a * jax.nn.sigmoid(g)
    y = causal_dwconv(y, dw_w) + dw_b
    y = layer_norm(y, ln_g, ln_b)
    return jax.nn.silu(y)


def token_mix(h, w_in, pool_w, pool_scale, sgu_ln_g, sgu_ln_b, sgu_w, sgu_b, sconv_w,
              conf_dw_w, conf_dw_b, conf_ln_g, conf_ln_b, gate_b, w_branch, w_out):
    bsz, s, _ = h.shape
    proj = h @ w_in
    a_in, b_in, c_in, d_in, g_in = jnp.split(proj, IN_SPLITS, axis=-1)
    ys = (pool_mixer(a_in, pool_w, pool_scale),
          sgu_mixer(b_in, sgu_ln_g, sgu_ln_b, sgu_w, sgu_b),
          short_conv_mixer(c_in, sconv_w),
          conformer_conv_mixer(d_in, conf_dw_w, conf_dw_b, conf_ln_g, conf_ln_b))
    gates = jax.nn.sigmoid(g_in + gate_b).reshape(bsz, s, N_BRANCH, D_MODEL)
    merged = None
    for i, (y, (r0, r1)) in enumerate(zip(ys, BRANCH_ROWS)):
        term = gates[:, :, i] * (y @ w_branch[r0:r1])
        merged = term if merged is None else merged + term
    return merged @ w_out


def conv_ffn(h, up, conv_w, down):
    z = causal_dwconv(h @ up, conv_w)
    gate, val = jnp.split(z, 2, axis=-1)
    return (jax.nn.silu(gate) * val) @ down


def setup_inputs(seed: int = 0) -> dict:
    key = jax.random.key(seed)
    ks = jax.random.split(key, 26)
    f32 = jnp.float32

    def nrm(k, shape, scale):
        return jax.random.normal(k, shape, f32) * scale

    L, D = DEPTH, D_MODEL
    return {
        "x": nrm(ks[0], (BATCH, SEQ, D), 1.0),
        "c": nrm(ks[1], (BATCH, D), 1.0),
        "ada_w": nrm(ks[2], (L, D, 6 * D), D ** -0.5),
        "ada_b": nrm(ks[3], (L, 6 * D), 0.02),
        "norm_mix_g": 1.0 + nrm(ks[4], (L, D), 0.02),
        "w_in": nrm(ks[5], (L, D, N_IN), D ** -0.5),
        "pool_w": nrm(ks[6], (L, N_POOL_GROUPS, POOL_GROUP_DIM, POOL_GROUP_DIM), POOL_GROUP_DIM ** -0.5),
        "pool_scale": 1.0 + nrm(ks[7], (L, W_A), 0.02),
        "sgu_ln_g": 1.0 + nrm(ks[8], (L, W_B), 0.02),
        "sgu_ln_b": nrm(ks[9], (L, W_B), 0.02),
        "sgu_w": nrm(ks[10], (L, SGU_HEADS, CHUNK, CHUNK), CHUNK ** -0.5),
        "sgu_b": 1.0 + nrm(ks[11], (L, SGU_HEADS, CHUNK), 0.02),
        "sconv_w": nrm(ks[12], (L, SHORT_CONV_K, W_C), SHORT_CONV_K ** -0.5),
        "conf_dw_w": nrm(ks[13], (L, CONF_CONV_K, W_D), CONF_CONV_K ** -0.5),
        "conf_dw_b": nrm(ks[14], (L, W_D), 0.02),
        "conf_ln_g": 1.0 + nrm(ks[15], (L, W_D), 0.02),
        "conf_ln_b": nrm(ks[16], (L, W_D), 0.02),
        "gate_b": nrm(ks[17], (L, N_BRANCH * D), 0.02),
        "w_branch": nrm(ks[18], (L, MIX_WIDTH, D), MIX_WIDTH ** -0.5),
        "w_out": nrm(ks[19], (L, D, D), D ** -0.5),
        "norm_ffn_g": 1.0 + nrm(ks[20], (L, D), 0.02),
        "ffn_up": nrm(ks[21], (L, D, 2 * FFN_HIDDEN), D ** -0.5),
        "ffn_conv": nrm(ks[22], (L, FFN_CONV_K, 2 * FFN_HIDDEN), FFN_CONV_K ** -0.5),
        "ffn_down": nrm(ks[23], (L, FFN_HIDDEN, D), FFN_HIDDEN ** -0.5),
        "final_g": 1.0 + nrm(ks[24], (D,), 0.02),
    }


def reference(x, c, ada_w, ada_b, norm_mix_g, w_in, pool_w, pool_scale, sgu_ln_g, sgu_ln_b,
              sgu_w, sgu_b, sconv_w, conf_dw_w, conf_dw_b, conf_ln_g, conf_ln_b, gate_b,
              w_branch, w_out, norm_ffn_g, ffn_up, ffn_conv, ffn_down, final_g):
    cond = jax.nn.silu(c)
    for l in range(DEPTH):
        mod = (cond @ ada_w[l] + ada_b[l])[:, None, :]
        sh_m, sc_m, g_m, sh_f, sc_f, g_f = jnp.split(mod, 6, axis=-1)
        h = rms_norm(x, norm_mix_g[l]) * (1.0 + sc_m) + sh_m
        x = x + g_m * token_mix(h, w_in[l], pool_w[l], pool_scale[l], sgu_ln_g[l], sgu_ln_b[l],
                                sgu_w[l], sgu_b[l], sconv_w[l], conf_dw_w[l], conf_dw_b[l],
                                conf_ln_g[l], conf_ln_b[l], gate_b[l], w_branch[l], w_out[l])
        h = rms_norm(x, norm_ffn_g[l]) * (1.0 + sc_f) + sh_f
        x = x + g_f * conv_ffn(h, ffn_up[l], ffn_conv[l], ffn_down[l])
    return rms_norm(x, final_g)
```

```python
import numpy as np
import concourse.bass as bass
import concourse.mybir as mybir
from concourse.bass_utils import run_bass_kernel_spmd

F32 = mybir.dt.float32
BF16 = mybir.dt.bfloat16
AF = mybir.ActivationFunctionType
ALU = mybir.AluOpType

RMS_EPS = 1e-6
LN_EPS = 1e-5
WINS = (2, 4, 8, 16)
CK = 31


class Cfg:
    def __init__(self, D=4096, S=2048, HID=11008, L=2, T=256):
        self.D, self.S, self.HID, self.L, self.T = D, S, HID, L, T
        self.DC = D // 128
        self.W = D // 4
        self.WC = self.W // 128
        self.GD = self.W // 4
        self.GC = self.GD // 128
        self.HC = HID // 128
        self.NIN = 8 * self.W + 4 * D
        self.NT = S // T
        self.HEADS = self.W // 128
        assert self.GD % 128 == 0 and HID % 256 == 0 and S % T == 0 and T % 128 == 0
        o = 0
        self.po = {}
        for name, n in (("nmg", self.DC), ("adab", 6 * self.DC), ("pscale", self.WC),
                        ("sconv", 3 * self.WC), ("cdw", CK * self.WC), ("cdb", self.WC),
                        ("clg", self.WC), ("clb", self.WC), ("gateb", 4 * self.DC),
                        ("nfg", self.DC), ("fconv", 3 * 2 * self.HC)):
            self.po[name] = o
            o += n
        self.NPL = o
        self.NP = o * L + self.DC
        self.KG = []
        k = 0
        while k < self.HC:
            n = min(32, self.HC - k)
            self.KG.append((k, n))
            k += n
        self.NBC = 3 * self.W


class Buf:
    __slots__ = ("w", "r")

    def __init__(self):
        self.w = None
        self.r = {}


class Eng:
    def __init__(self, name, sem, is_pe=False):
        self.name, self.sem, self.is_pe = name, sem, is_pe
        self.count = 0
        self.known = {}
        self.ops = []


class Prog:
    NDS = 24

    def __init__(self, nc, sems, dsems):
        self.nc = nc
        self.pe = Eng("pe", sems[0], True)
        self.act = Eng("act", sems[1])
        self.dve = Eng("dve", sems[2])
        self.pool = Eng("pool", sems[3])
        self.sp = Eng("sp", None)
        self.dsems = dsems
        self.dvals = [0] * len(dsems)
        self.dma_i = 0
        self.out_toks = []

    def _deps(self, e, reads, writes, extra=()):
        deps = {}

        def add(tok):
            if tok is None:
                return
            k = id(tok[0])
            if k not in deps or deps[k][1] < tok[1]:
                deps[k] = tok
        for b in reads:
            add(b.w)
        for b in writes:
            add(b.w)
            for t in b.r.values():
                add(t)
        for t in extra:
            add(t)
        for k, (sem, val) in deps.items():
            if e.is_pe and sem is e.sem:
                continue
            if e.known.get(k, 0) >= val:
                continue
            e.known[k] = val
            e.ops.append(("wait", sem, val))

    def _mark(self, tok, reads, writes):
        for b in writes:
            b.w = tok
            b.r = {}
        k = id(tok[0])
        for b in reads:
            if b.w is not tok:
                b.r[k] = tok

    def op(self, e, fn, reads=(), writes=()):
        self._deps(e, reads, writes)
        e.count += 1
        tok = (e.sem, e.count)
        e.ops.append(("op", fn, True))
        self._mark(tok, reads, writes)

    def group(self, e, fns, reads=(), writes=()):
        self._deps(e, reads, writes)
        e.count += 1
        tok = (e.sem, e.count)
        n = len(fns)
        for i, fn in enumerate(fns):
            e.ops.append(("op", fn, i == n - 1))
        self._mark(tok, reads, writes)

    def dma(self, q, out_ap, in_ap, reads=(), writes=(), is_out=False):
        i = self.dma_i % len(self.dsems)
        self.dma_i += 1
        s = self.dsems[i]
        prev = self.dvals[i]
        self.dvals[i] += 16
        val = self.dvals[i]
        extra = [(s, prev)] if prev > 0 else []
        self._deps(q, reads, writes, extra)
        q.ops.append(("dma", out_ap, in_ap, s))
        tok = (s, val)
        self._mark(tok, reads, writes)
        if is_out:
            self.out_toks.append(tok)

    def emit(self, e, h):
        for o in e.ops:
            if o[0] == "wait":
                h.wait_ge(o[1], o[2])
            elif o[0] == "op":
                ins = o[1](h)
                if o[2]:
                    ins.then_inc(e.sem, 1)
            else:
                h.dma_start(out=o[1], in_=o[2]).then_inc(o[3], 16)


def build_nc(cfg):
    D, S, HID, L, T = cfg.D, cfg.S, cfg.HID, cfg.L, cfg.T
    DC, W, WC, GD, GC, HC, NIN, NT, HEADS = (cfg.DC, cfg.W, cfg.WC, cfg.GD, cfg.GC, cfg.HC,
                                             cfg.NIN, cfg.NT, cfg.HEADS)
    nc = bass.Bass("TRN2", target_bir_lowering=False)

    def din(name, shape, dt=F32):
        return nc.dram_tensor(name, list(shape), dt, kind="ExternalInput").ap()

    xT = din("xT", [D, S])
    cT = din("cT", [128, DC])
    ada_w = din("ada_w", [L * D, 6 * D])
    w_in = din("w_in", [L * D, NIN])
    w_branch = din("w_branch", [L * D, D])
    w_out = din("w_out", [L * D, D])
    ffn_up = din("ffn_up", [L * D, 2 * HID])
    ffn_down = din("ffn_down", [L * HID, D])
    pool_w = din("pool_w", [L * 4 * GD, GD])
    params_d = din("params", [128, cfg.NP])
    bcast_d = din("bcast", [L * 128, cfg.NBC])
    consts_d = din("consts", [128, 64 + 128 + 128])
    sguT = din("sguT", [L * HEADS * 128, 128])
    outT = nc.dram_tensor("outT", [D, S], F32, kind="ExternalOutput").ap()

    def dint(name, n, kc):
        return nc.dram_tensor(name, [n, 128, kc * 128], BF16, kind="Internal").ap()

    class Slabs:
        def __init__(self, name, n, kc, ngroups):
            self.ap = dint(name, n, kc)
            self.kc = kc
            self.bufs = [[Buf() for _ in range(ngroups)] for _ in range(n)]

    def ng(kc):
        return (kc + 7) // 8

    S_in = [Slabs(f"s_in{l}", NIN // 128, DC, ng(DC)) for l in range(L)]
    S_br = [Slabs(f"s_br{l}", 4 * DC, WC, ng(WC)) for l in range(L)]
    S_out = [Slabs(f"s_out{l}", DC, DC, ng(DC)) for l in range(L)]
    S_up = [Slabs(f"s_up{l}", 2 * HC, DC, ng(DC)) for l in range(L)]
    S_dn = [[Slabs(f"s_dn{l}_{g}", DC, n, ng(n)) for g, (k0, n) in enumerate(cfg.KG)]
            for l in range(L)]
    S_pw = [Slabs(f"s_pw{l}", 4 * GC, GC, ng(GC)) for l in range(L)]

    from contextlib import ExitStack
    es = ExitStack()

    def sb(name, shape, dt):
        return es.enter_context(nc.sbuf_tensor(name, list(shape), dt))

    TW = T + 32
    NTMP = 14
    NSLOT = 4
    x_sb = sb("x_sb", [128, DC * T], F32)
    h_sb = sb("h_sb", [128, DC * T], BF16)
    RAU = max(HC, 4 * WC + DC, 80)
    ra32 = sb("ra", [128, RAU * T // 2], F32)
    ra16 = ra32[:, :].bitcast(BF16)
    slots = [sb(f"slot{i}", [128, 32 * 128], BF16) for i in range(NSLOT)]
    scr = sb("scr", [128, max(2 * W, WC * T, 2048)], F32)
    NSCR = max(2 * W, WC * T, 2048) // T
    vT = sb("vT", [128, 2 * W], BF16)
    ydt = sb("ydt", [128, WC * T], F32)
    bc = sb("bc", [128, cfg.NBC], F32)
    cst = sb("cst", [128, 64 + 128 + 128], F32)
    par = sb("par", [128, cfg.NP], F32)
    modd = sb("modd", [128, L * 6 * DC], F32)
    tmps = [sb(f"tmp{i}", [128, TW], F32) for i in range(NTMP)]
    pl = [sb(f"pl{i}", [128, T], BF16) for i in range(2 * GC)]
    wm = sb("wm", [128, L * HEADS * 128], BF16)
    ones32 = sb("ones32", [128, 128], F32)
    condT = sb("condT", [128, DC], F32)
    rowsb = [sb(f"rowsb{i}", [1, 256], F32) for i in range(2)]
    smalls = sb("smalls", [128, 64], F32)
    rstd_t = sb("rstd_t", [128, T], F32)
    carA = sb("carA", [128, L * WC * 16], F32)
    carC = sb("carC", [128, L * WC * 2], F32)
    carD = sb("carD", [128, L * WC * 30], F32)
    carF = sb("carF", [128, L * 2 * HC * 2], F32)
    pst = [es.enter_context(nc.psum_tensor(f"ps{i}", [128, 512], F32)) for i in range(8)]

    sems = [es.enter_context(nc.semaphore(f"se{i}")) for i in range(4)]
    dsems = [es.enter_context(nc.semaphore(f"sd{i}")) for i in range(Prog.NDS)]
    P = Prog(nc, sems, dsems)
    PE, ACT, DVE, POOL, SP = P.pe, P.act, P.dve, P.pool, P.sp

    xb = [Buf() for _ in range(DC)]
    hb = [Buf() for _ in range(DC)]
    rab = [Buf() for _ in range(RAU)]
    slotb = [Buf() for _ in range(NSLOT)]
    scrb = [Buf() for _ in range(NSCR)]
    vTb = [Buf() for _ in range(2)]
    ydb = [Buf() for _ in range(WC)]
    bcb, cstb, parb, wmb, onesb, condb, smallb = Buf(), Buf(), Buf(), Buf(), Buf(), Buf(), Buf()
    modb = [Buf() for _ in range(L)]
    tmpb = [Buf() for _ in range(NTMP)]
    plb = [Buf() for _ in range(2 * GC)]
    rowb = [Buf(), Buf()]
    carAb, carCb, carDb, carFb = Buf(), Buf(), Buf(), Buf()
    rstdb = Buf()
    psb = [Buf() for _ in range(16)]
    st = {"ps": 0, "tmp": 0, "slot": 0, "cast": 0}

    def psum(reserved=None):
        if reserved is None:
            i = st["ps"] % 7
            st["ps"] += 1
        else:
            i = 7
        return pst[i][:, 0:256], psb[i]

    def tmp():
        i = st["tmp"] % NTMP
        st["tmp"] += 1
        return tmps[i], tmpb[i]

    def xs(kc):
        return x_sb[:, kc * T:(kc + 1) * T]

    def hs(kc):
        return h_sb[:, kc * T:(kc + 1) * T]

    def ra(u):
        return ra16[:, u * T:(u + 1) * T]

    def pcol(l, name, j):
        o = l * cfg.NPL + cfg.po[name] + j
        return par[:, o:o + 1]

    def mcol(l, k, j):
        o = (l * 6 + k) * DC + j
        return modd[:, o:o + 1]

    NST32, NST16 = 4, 4
    st32 = [ra32[:, i * 2048:(i + 1) * 2048] for i in range(4)]
    st32b = [rab[i * 16:(i + 1) * 16] for i in range(4)]
    scr16 = scr[:, :].bitcast(BF16)
    st16 = [ra16[:, 16384 + i * 2048: 16384 + (i + 1) * 2048] for i in range(2)] + \
           [scr16[:, i * 2048:(i + 1) * 2048] for i in range(2)]
    u16 = 4096 // (T * 4)
    st16b = [rab[64 + i * 8: 64 + (i + 1) * 8] for i in range(2)] + [scrb[i * u16:(i + 1) * u16] for i in range(2)]
    assert RAU >= 80 and NSCR * T * 4 >= 8192

    P.dma(ACT, par[:, :], params_d, writes=[parb])
    P.dma(ACT, cst[:, :], consts_d, writes=[cstb])
    P.dma(ACT, condT[:, :], cT, writes=[condb])
    P.op(ACT, lambda e: e.activation(out=condT[:, :], in_=condT[:, :], func=AF.Silu), writes=[condb])
    P.op(DVE, lambda e: e.tensor_copy(out=ones32[:, :], in_=cst[:, 192:320]), reads=[cstb], writes=[onesb])
    for cb, ct in ((carAb, carA), (carCb, carC), (carDb, carD), (carFb, carF)):
        P.op(POOL, lambda e, ct=ct: e.memset(ct[:, :], 0.0), writes=[cb])

    stc = {"i": 0, "o": 0}

    def stage_in(src_ap, nk, ncol):
        i = stc["i"] % NST32
        stc["i"] += 1
        v = st32[i].rearrange("p (k c) -> p k c", c=256)[:, 0:nk, 0:ncol]
        P.dma(SP, v, src_ap.rearrange("(k p) c -> p k c", p=128), writes=st32b[i])
        return i, v

    cast_engs = [DVE, POOL, ACT]

    def cast(out_ap, in_ap, reads, writes):
        e = cast_engs[st["cast"] % 3]
        st["cast"] += 1
        if e is ACT:
            P.op(e, lambda g: g.activation(out=out_ap, in_=in_ap, func=AF.Copy), reads=reads, writes=writes)
        else:
            P.op(e, lambda g: g.tensor_copy(out=out_ap, in_=in_ap), reads=reads, writes=writes)

    pending = []
    LAG = 2

    def flush_stores(n_keep):
        while len(pending) > n_keep:
            a = pending.pop(0)
            P.dma(SP, a[0], a[1], reads=a[2], writes=a[3])

    def precast(src, K, N, slabs, slab_base):
        KC = K // 128
        MC = N // 128
        m = 0
        oi = 0
        while m < MC:
            nm = 2 if m + 1 < MC else 1
            for g in range(ng(KC)):
                k0 = g * 8
                nk = min(8, KC - k0)
                i, v = stage_in(src[k0 * 128:(k0 + nk) * 128, m * 128:(m + nm) * 128], nk, nm * 128)
                j = stc["o"] % NST16
                stc["o"] += 1
                o16 = st16[j].rearrange("p (m k c) -> p m k c", m=2, c=128)
                for mm in range(nm):
                    cast(o16[:, mm, 0:nk, :], v[:, :, mm * 128:(mm + 1) * 128], st32b[i], st16b[j])
                dst = slabs.ap[slab_base + m:slab_base + m + nm, :, k0 * 128:(k0 + nk) * 128]
                pending.append((dst.rearrange("m p (k c) -> p m k c", c=128), o16[:, 0:nm, 0:nk, :],
                                st16b[j], [slabs.bufs[slab_base + m + mm][g] for mm in range(nm)]))
                flush_stores(LAG)
            m += nm

    def ada_layer(l):
        pT, pTb = psum(reserved=0)
        import os
        KA = int(os.environ.get("KADA", "9"))
        for s in range(6 * D // 256):
            prow, prb = psum()
            for g in range(ng(DC)):
                k0 = g * 8
                nk = min(8, DC - k0)
                i, v = stage_in(ada_w[l * D + k0 * 128: l * D + (k0 + nk) * 128, s * 256:(s + 1) * 256], nk, 256)
                fns = [(lambda e, kk=kk, v=v, prow=prow, k0=k0:
                        e.matmul(prow[0:1, :], lhsT=condT[:, k0 + kk:k0 + kk + 1], rhs=v[:, kk, :],
                                 start=(k0 + kk == 0), stop=(k0 + kk == DC - 1))) for kk in range(nk)]
                if KA >= 2:
                    P.group(PE, fns, reads=st32b[i] + [condb], writes=[prb])
            r = s % 2
            if KA < 3:
                continue
            P.op(ACT, lambda e, r=r, prow=prow: e.activation(out=rowsb[r][0:1, :], in_=prow[0:1, :], func=AF.Copy),
                 reads=[prb], writes=[rowb[r]])
            fns = [(lambda e, mm=mm, r=r, s=s:
                    e.matmul(pT[:, 2 * s + mm:2 * s + mm + 1], lhsT=rowsb[r][0:1, mm * 128:(mm + 1) * 128],
                             rhs=ones32[0:1, 0:1], start=True, stop=True)) for mm in range(2)]
            if KA >= 4:
                P.group(PE, fns, reads=[rowb[r], onesb], writes=[pTb])
        if KA < 5:
            return
        o = l * cfg.NPL + cfg.po["adab"]
        raw, rawb = tmp()
        P.op(DVE, lambda e: e.tensor_tensor(out=raw[:, 0:6 * DC], in0=pT[:, 0:6 * DC], in1=par[:, o:o + 6 * DC], op=ALU.add),
             reads=[pTb, parb], writes=[rawb])
        base = l * 6 * DC
        og = l * cfg.NPL + cfg.po["nmg"]
        of = l * cfg.NPL + cfg.po["nfg"]
        P.op(DVE, lambda e: e.scalar_tensor_tensor(out=modd[:, base:base + DC], in0=raw[:, DC:2 * DC], scalar=1.0,
                                                   in1=par[:, og:og + DC], op0=ALU.add, op1=ALU.mult),
             reads=[rawb, parb], writes=[modb[l]])
        P.op(DVE, lambda e: e.tensor_copy(out=modd[:, base + DC:base + 2 * DC], in_=raw[:, 0:DC]), reads=[rawb], writes=[modb[l]])
        P.op(DVE, lambda e: e.tensor_copy(out=modd[:, base + 2 * DC:base + 3 * DC], in_=raw[:, 2 * DC:3 * DC]), reads=[rawb], writes=[modb[l]])
        P.op(DVE, lambda e: e.scalar_tensor_tensor(out=modd[:, base + 3 * DC:base + 4 * DC], in0=raw[:, 4 * DC:5 * DC], scalar=1.0,
                                                   in1=par[:, of:of + DC], op0=ALU.add, op1=ALU.mult),
             reads=[rawb, parb], writes=[modb[l]])
        P.op(DVE, lambda e: e.tensor_copy(out=modd[:, base + 4 * DC:base + 5 * DC], in_=raw[:, 3 * DC:4 * DC]), reads=[rawb], writes=[modb[l]])
        P.op(DVE, lambda e: e.tensor_copy(out=modd[:, base + 5 * DC:base + 6 * DC], in_=raw[:, 5 * DC:6 * DC]), reads=[rawb], writes=[modb[l]])

    import os
    DBG = os.environ.get("KDBG", "")
    for l in range(L):
        if "noada" not in DBG:
            ada_layer(l)
        i, v = stage_in(sguT[l * HEADS * 128:(l + 1) * HEADS * 128, :], HEADS, 128)
        for hh in range(HEADS):
            o = (l * HEADS + hh) * 128
            P.op(DVE, lambda e, v=v, hh=hh, o=o: e.tensor_tensor(out=wm[:, o:o + 128], in0=v[:, hh, :], in1=cst[:, 64:192], op=ALU.mult),
                 reads=st32b[i] + [cstb], writes=[wmb])
    for l in range(L):
        if "nocast" in DBG:
            break
        precast(w_in[l * D:(l + 1) * D, :], D, NIN, S_in[l], 0)
        for g in range(4):
            precast(pool_w[(l * 4 + g) * GD:(l * 4 + g + 1) * GD, :], GD, GD, S_pw[l], g * GC)
        for i in range(4):
            precast(w_branch[l * D + i * W: l * D + (i + 1) * W, :], W, D, S_br[l], i * DC)
        precast(w_out[l * D:(l + 1) * D, :], D, D, S_out[l], 0)
        precast(ffn_up[l * D:(l + 1) * D, :], D, 2 * HID, S_up[l], 0)
        for g, (k0, n) in enumerate(cfg.KG):
            precast(ffn_down[l * HID + k0 * 128: l * HID + (k0 + n) * 128, :], n * 128, D, S_dn[l][g], 0)

    flush_stores(0)

    def load_slab(slabs, idx):
        i = st["slot"] % NSLOT
        st["slot"] += 1
        n = slabs.kc * 128
        P.dma(SP, slots[i][:, 0:n], slabs.ap[idx], reads=slabs.bufs[idx], writes=[slotb[i]])
        return slots[i], slotb[i]

    def proj(slabs, idx, rhs_fn, rhs_bufs, nk=None):
        sl, slb = load_slab(slabs, idx)
        nk = slabs.kc
        ps, pb = psum()
        fns = [(lambda e, k=k: e.matmul(ps, lhsT=sl[:, k * 128:(k + 1) * 128], rhs=rhs_fn(k),
                                        start=(k == 0), stop=(k == nk - 1))) for k in range(nk)]
        P.group(PE, fns, reads=[slb] + rhs_bufs, writes=[pb])
        return ps, pb

    def rsqrt_inplace(ap, buf):
        P.op(ACT, lambda e: e.activation(out=ap, in_=ap, func=AF.Sqrt), writes=[buf])
        P.op(DVE, lambda e: e.reciprocal(out=ap, in_=ap), writes=[buf])

    def rms_to_h(Acol, Bcol):
        ps, pb = psum()
        for kc in range(DC):
            sq, sqb = tmp()
            P.op(ACT, lambda e, kc=kc, sq=sq: e.activation(out=sq[:, 0:T], in_=xs(kc), func=AF.Square),
                 reads=[xb[kc]], writes=[sqb])
            P.group(PE, [lambda e, kc=kc, sq=sq: e.matmul(ps, lhsT=ones32[:, :], rhs=sq[:, 0:T],
                                                         start=(kc == 0), stop=(kc == DC - 1))],
                    reads=[sqb, onesb], writes=[pb])
        rstd, rsb = rstd_t, rstdb
        P.op(DVE, lambda e: e.tensor_scalar(out=rstd[:, 0:T], in0=ps, scalar1=1.0 / D, scalar2=RMS_EPS,
                                            op0=ALU.mult, op1=ALU.add), reads=[pb], writes=[rsb])
        rsqrt_inplace(rstd[:, 0:T], rsb)
        return rstd, rsb

    def norm_mod(l, ka, kb):
        rstd, rsb = rms_to_h(None, None)
        for kc in range(DC):
            t, tb = tmp()
            P.op(DVE, lambda e, kc=kc, t=t: e.tensor_tensor(out=t[:, 0:T], in0=xs(kc), in1=rstd[:, 0:T], op=ALU.mult),
                 reads=[xb[kc], rsb], writes=[tb])
            P.op(ACT, lambda e, kc=kc, t=t: e.activation(out=hs(kc), in_=t[:, 0:T], func=AF.Identity,
                                                         bias=mcol(l, kb, kc), scale=mcol(l, ka, kc)),
                 reads=[tb, modb[l]], writes=[hb[kc]])

    hfn = lambda k: hs(k)

    KS = int(os.environ.get("KSTG", "99"))

    def layer(l, ti):
        first = (ti == 0)
        P.dma(ACT, bc[:, :], bcast_d[l * 128:(l + 1) * 128, :], writes=[bcb])
        norm_mod(l, 0, 1)
        YA, YB, YC, YD = 0, WC, 2 * WC, 3 * WC
        MG = 4 * WC
        if KS < 4:
            return
        yu = 1
        for c in range(WC):
            ps_a, pab = proj(S_in[l], 6 * WC + c, hfn, hb)
            ps_g, pgb = proj(S_in[l], 7 * WC + c, hfn, hb)
            sg, sgb = tmp()
            P.op(ACT, lambda e, sg=sg, ps_g=ps_g: e.activation(out=sg[:, 0:T], in_=ps_g, func=AF.Sigmoid), reads=[pgb], writes=[sgb])
            yb, ybb = tmp()
            co = (l * WC + c) * 30
            P.op(POOL, lambda e, yb=yb, co=co: e.tensor_copy(out=yb[:, 0:30], in_=carD[:, co:co + 30]), reads=[carDb], writes=[ybb])
            P.op(DVE, lambda e, yb=yb, sg=sg, ps_a=ps_a: e.tensor_tensor(out=yb[:, 30:30 + T], in0=sg[:, 0:T], in1=ps_a, op=ALU.mult),
                 reads=[sgb, pab], writes=[ybb])
            P.op(POOL, lambda e, yb=yb, co=co: e.tensor_copy(out=carD[:, co:co + 30], in_=yb[:, T:T + 30]), reads=[ybb], writes=[carDb])
            yd = ydt[:, c * T:(c + 1) * T]
            P.op(DVE, lambda e, yd=yd, yb=yb, c=c: e.tensor_scalar(out=yd, in0=yb[:, 0:T], scalar1=pcol(l, "cdw", c),
                                                                 scalar2=pcol(l, "cdb", c), op0=ALU.mult, op1=ALU.add),
                 reads=[ybb, parb], writes=[ydb[c]])
            for k in range(1, CK):
                P.op(DVE, lambda e, yd=yd, yb=yb, c=c, k=k: e.scalar_tensor_tensor(
                    out=yd, in0=yb[:, k:k + T], scalar=pcol(l, "cdw", k * WC + c), in1=yd, op0=ALU.mult, op1=ALU.add),
                    reads=[ybb, parb], writes=[ydb[c]])
        ps_s, pssb = psum()
        ps_q, psqb = psum()
        for c in range(WC):
            yd = ydt[:, c * T:(c + 1) * T]
            P.group(PE, [lambda e, yd=yd, c=c: e.matmul(ps_s, lhsT=ones32[:, :], rhs=yd, start=(c == 0), stop=(c == WC - 1))],
                    reads=[ydb[c], onesb], writes=[pssb])
            sq, sqb = tmp()
            P.op(ACT, lambda e, sq=sq, yd=yd: e.activation(out=sq[:, 0:T], in_=yd, func=AF.Square), reads=[ydb[c]], writes=[sqb])
            P.group(PE, [lambda e, sq=sq, c=c: e.matmul(ps_q, lhsT=ones32[:, :], rhs=sq[:, 0:T], start=(c == 0), stop=(c == WC - 1))],
                    reads=[sqb, onesb], writes=[psqb])
        mu, mub = tmp()
        rs, rsb = tmp()
        P.op(DVE, lambda e: e.tensor_scalar(out=mu[:, 0:T], in0=ps_s, scalar1=1.0 / W, scalar2=None, op0=ALU.mult), reads=[pssb], writes=[mub])
        P.op(DVE, lambda e: e.tensor_tensor(out=rs[:, 0:T], in0=mu[:, 0:T], in1=mu[:, 0:T], op=ALU.mult), reads=[mub], writes=[rsb])
        P.op(DVE, lambda e: e.scalar_tensor_tensor(out=rs[:, 0:T], in0=ps_q, scalar=1.0 / W, in1=rs[:, 0:T], op0=ALU.mult, op1=ALU.subtract),
             reads=[psqb], writes=[rsb])
        P.op(DVE, lambda e: e.tensor_scalar(out=rs[:, 0:T], in0=rs[:, 0:T], scalar1=LN_EPS, scalar2=None, op0=ALU.add), writes=[rsb])
        rsqrt_inplace(rs[:, 0:T], rsb)
        for c in range(WC):
            yd = ydt[:, c * T:(c + 1) * T]
            P.op(DVE, lambda e, yd=yd: e.tensor_tensor(out=yd, in0=yd, in1=mu[:, 0:T], op=ALU.subtract), reads=[mub], writes=[ydb[c]])
            P.op(DVE, lambda e, yd=yd: e.tensor_tensor(out=yd, in0=yd, in1=rs[:, 0:T], op=ALU.mult), reads=[rsb], writes=[ydb[c]])
            P.op(ACT, lambda e, yd=yd, c=c: e.activation(out=ra(YD + c), in_=yd, func=AF.Silu, bias=pcol(l, "clb", c), scale=pcol(l, "clg", c)),
                 reads=[ydb[c], parb], writes=[rab[YD + c]])
        if KS < 1:
            return
        for g in range(4):
            w = WINS[g]
            for c2 in range(GC):
                c = g * GC + c2
                ps, pb = proj(S_in[l], c, hfn, hb)
                KB = int(os.environ.get("KA2", "9"))
                if KB < 2:
                    continue
                ab, abb = tmp()
                co = (l * WC + c) * 16
                P.op(POOL, lambda e, ab=ab, co=co: e.tensor_copy(out=ab[:, 0:16], in_=carA[:, co:co + 16]), reads=[carAb], writes=[abb])
                P.op(ACT, lambda e, ab=ab, ps=ps: e.activation(out=ab[:, 16:16 + T], in_=ps, func=AF.Copy), reads=[pb], writes=[abb])
                P.op(POOL, lambda e, ab=ab, co=co: e.tensor_copy(out=carA[:, co:co + 16], in_=ab[:, T:T + 16]), reads=[abb], writes=[carAb])
                if KB < 3:
                    continue
                WB = 16 + T
                cur, curb = ab, abb
                lag = 1
                while lag < w:
                    nx, nxb = tmp()
                    lo = 2 * lag - 1
                    P.op(DVE, lambda e, nx=nx, cur=cur, lag=lag, lo=lo: e.tensor_tensor(
                        out=nx[:, lo:WB], in0=cur[:, lo:WB], in1=cur[:, lo - lag:WB - lag], op=ALU.add),
                        reads=[curb], writes=[nxb])
                    cur, curb = nx, nxb
                    lag *= 2
                pi = c2
                if KB < 4:
                    continue
                P.op(DVE, lambda e, cur=cur, ab=ab, pi=pi, w=w: e.scalar_tensor_tensor(
                    out=pl[pi][:, :], in0=cur[:, 16:16 + T], scalar=1.0 / w, in1=ab[:, 16:16 + T],
                    op0=ALU.mult, op1=ALU.subtract), reads=[curb, abb], writes=[plb[pi]])
                if first and KB >= 5:
                    t16, t16b = tmp()
                    P.op(DVE, lambda e, t16=t16, cur=cur, g=g: e.tensor_tensor(
                        out=t16[:, 0:16], in0=cur[:, 16:32], in1=cst[:, g * 16:(g + 1) * 16], op=ALU.mult),
                        reads=[curb, cstb], writes=[t16b])
                    P.op(DVE, lambda e, t16=t16, ab=ab, pi=pi: e.tensor_tensor(
                        out=pl[pi][:, 0:16], in0=t16[:, 0:16], in1=ab[:, 16:32], op=ALU.subtract),
                        reads=[t16b, abb], writes=[plb[pi]])
            for m2 in range(GC):
                if int(os.environ.get("KA2", "9")) < 6:
                    continue
                c = g * GC + m2
                ps, pb = proj(S_pw[l], g * GC + m2, lambda k: pl[k][:, :], plb[0:GC])
                P.op(ACT, lambda e, ps=ps, c=c: e.activation(out=ra(YA + c), in_=ps, func=AF.Copy, scale=pcol(l, "pscale", c)),
                     reads=[pb, parb], writes=[rab[YA + c]])
        if KS < 2:
            return
        for c in range(WC):
            ps, pb = proj(S_in[l], WC + c, hfn, hb)
            P.op(ACT, lambda e, ps=ps, c=c: e.activation(out=ra(YB + c), in_=ps, func=AF.Gelu), reads=[pb], writes=[rab[YB + c]])
        NS2 = T // 128
        su = W // T if W >= T else 1
        for j in range(WC):
            sl, slb = load_slab(S_in[l], 2 * WC + j)
            ps, pb = psum()
            fns = []
            for s2 in range(NS2):
                for k in range(DC):
                    fns.append(lambda e, s2=s2, k=k, sl=sl, ps=ps: e.matmul(
                        ps[:, s2 * 128:(s2 + 1) * 128], lhsT=h_sb[:, k * T + s2 * 128:k * T + (s2 + 1) * 128],
                        rhs=sl[:, k * 128:(k + 1) * 128], start=(k == 0), stop=(k == DC - 1)))
            P.group(PE, fns, reads=[slb] + hb, writes=[pb])
            for s2 in range(NS2):
                P.op(ACT, lambda e, s2=s2, j=j, ps=ps: e.activation(
                    out=scr[:, s2 * W + j * 128:s2 * W + (j + 1) * 128], in_=ps[:, s2 * 128:(s2 + 1) * 128], func=AF.Gelu),
                    reads=[pb], writes=scrb[s2 * su:(s2 + 1) * su])
        for s2 in range(NS2):
            vb = scr[:, s2 * W:(s2 + 1) * W]
            vbb = scrb[s2 * su:(s2 + 1) * su]
            nst = (W + 511) // 512
            so = 0
            for q in range(nst):
                a0, a1 = q * 512, min(W, (q + 1) * 512)
                P.op(DVE, lambda e, q=q, a0=a0, a1=a1, vb=vb: e.bn_stats(out=smalls[:, q * 6:(q + 1) * 6], in_=vb[:, a0:a1]),
                     reads=vbb, writes=[smallb])
            P.op(DVE, lambda e: e.bn_aggr(out=smalls[:, 32:34], in_=smalls[:, 0:nst * 6]), writes=[smallb])
            P.op(DVE, lambda e: e.tensor_scalar(out=smalls[:, 34:35], in0=smalls[:, 33:34], scalar1=LN_EPS, scalar2=None,
                                                op0=ALU.add), writes=[smallb])
            rsqrt_inplace(smalls[:, 34:35], smallb)
            P.op(DVE, lambda e, vb=vb: e.tensor_scalar(out=vb, in0=vb, scalar1=smalls[:, 32:33], scalar2=smalls[:, 34:35],
                                                       op0=ALU.subtract, op1=ALU.mult), reads=[smallb], writes=vbb)
            P.op(DVE, lambda e, vb=vb: e.tensor_tensor(out=vb, in0=vb, in1=bc[:, 0:W], op=ALU.mult), reads=[bcb], writes=vbb)
            P.op(DVE, lambda e, vb=vb, s2=s2: e.tensor_tensor(out=vT[:, s2 * W:(s2 + 1) * W], in0=vb, in1=bc[:, W:2 * W], op=ALU.add),
                 reads=[bcb] + vbb, writes=[vTb[s2]])
        for hh in range(HEADS):
            ps, pb = psum()
            o = (l * HEADS + hh) * 128
            fns = [lambda e, s2=s2, hh=hh, ps=ps, o=o: e.matmul(
                ps[:, s2 * 128:(s2 + 1) * 128], lhsT=vT[:, s2 * W + hh * 128:s2 * W + (hh + 1) * 128],
                rhs=wm[:, o:o + 128], start=True, stop=True) for s2 in range(NS2)]
            P.group(PE, fns, reads=[vTb[0], vTb[1], wmb], writes=[pb])
            t, tb = tmp()
            for s2 in range(NS2):
                P.op(DVE, lambda e, s2=s2, hh=hh, ps=ps, t=t: e.tensor_tensor(
                    out=t[:, s2 * 128:(s2 + 1) * 128], in0=ps[:, s2 * 128:(s2 + 1) * 128],
                    in1=bc[:, 2 * W + hh * 128:2 * W + (hh + 1) * 128], op=ALU.add), reads=[pb, bcb], writes=[tb])
            P.op(DVE, lambda e, hh=hh, t=t: e.tensor_tensor(out=ra(YB + hh), in0=t[:, 0:T], in1=ra(YB + hh), op=ALU.mult),
                 reads=[tb], writes=[rab[YB + hh]])
        if KS < 3:
            return
        for c in range(WC):
            ps_b, pbb = proj(S_in[l], 3 * WC + c, hfn, hb)
            ps_c, pcb = proj(S_in[l], 4 * WC + c, hfn, hb)
            ps_h, phb = proj(S_in[l], 5 * WC + c, hfn, hb)
            cg, cgb = tmp()
            P.op(ACT, lambda e, cg=cg, ps_c=ps_c: e.activation(out=cg[:, 0:T], in_=ps_c, func=AF.Copy), reads=[pcb], writes=[cgb])
            pr, prb = tmp()
            co = (l * WC + c) * 2
            P.op(POOL, lambda e, pr=pr, co=co: e.tensor_copy(out=pr[:, 0:2], in_=carC[:, co:co + 2]), reads=[carCb], writes=[prb])
            P.op(DVE, lambda e, pr=pr, cg=cg, ps_h=ps_h: e.tensor_tensor(out=pr[:, 2:2 + T], in0=cg[:, 0:T], in1=ps_h, op=ALU.mult),
                 reads=[cgb, phb], writes=[prb])
            P.op(POOL, lambda e, pr=pr, co=co: e.tensor_copy(out=carC[:, co:co + 2], in_=pr[:, T:T + 2]), reads=[prb], writes=[carCb])
            ac, acb = tmp()
            P.op(DVE, lambda e, ac=ac, pr=pr, c=c: e.tensor_scalar(out=ac[:, 0:T], in0=pr[:, 0:T], scalar1=pcol(l, "sconv", 0 * WC + c),
                                                                 scalar2=None, op0=ALU.mult), reads=[prb, parb], writes=[acb])
            for k in (1, 2):
                P.op(DVE, lambda e, ac=ac, pr=pr, c=c, k=k: e.scalar_tensor_tensor(
                    out=ac[:, 0:T], in0=pr[:, k:k + T], scalar=pcol(l, "sconv", k * WC + c), in1=ac[:, 0:T],
                    op0=ALU.mult, op1=ALU.add), reads=[prb, parb], writes=[acb])
            P.op(DVE, lambda e, ac=ac, ps_b=ps_b, c=c: e.tensor_tensor(out=ra(YC + c), in0=ac[:, 0:T], in1=ps_b, op=ALU.mult),
                 reads=[acb, pbb], writes=[rab[YC + c]])
        if KS < 5:
            return
        for m in range(DC):
            acc, accb = tmp()
            for i in range(4):
                ps_g, pgb = proj(S_in[l], 8 * WC + i * DC + m, hfn, hb)
                ps_b, pbb = proj(S_br[l], i * DC + m, lambda k, i=i: ra(i * WC + k), rab[i * WC:(i + 1) * WC])
                sg, sgb = tmp()
                P.op(ACT, lambda e, sg=sg, ps_g=ps_g, i=i, m=m: e.activation(out=sg[:, 0:T], in_=ps_g, func=AF.Sigmoid,
                                                                             bias=pcol(l, "gateb", i * DC + m)),
                     reads=[pgb, parb], writes=[sgb])
                if i == 0:
                    P.op(DVE, lambda e, acc=acc, sg=sg, ps_b=ps_b: e.tensor_tensor(out=acc[:, 0:T], in0=sg[:, 0:T], in1=ps_b, op=ALU.mult),
                         reads=[sgb, pbb], writes=[accb])
                else:
                    P.op(DVE, lambda e, sg=sg, ps_b=ps_b: e.tensor_tensor(out=sg[:, 0:T], in0=sg[:, 0:T], in1=ps_b, op=ALU.mult),
                         reads=[pbb], writes=[sgb])
                    if i < 3:
                        P.op(POOL, lambda e, acc=acc, sg=sg: e.tensor_tensor(out=acc[:, 0:T], in0=acc[:, 0:T], in1=sg[:, 0:T], op=ALU.add),
                             reads=[sgb], writes=[accb])
                    else:
                        P.op(POOL, lambda e, acc=acc, sg=sg, m=m: e.tensor_tensor(out=ra(MG + m), in0=acc[:, 0:T], in1=sg[:, 0:T], op=ALU.add),
                             reads=[sgb, accb], writes=[rab[MG + m]])
        if KS < 6:
            return
        for m in range(DC):
            ps, pb = proj(S_out[l], m, lambda k: ra(MG + k), rab[MG:MG + DC])
            P.op(DVE, lambda e, ps=ps, m=m: e.scalar_tensor_tensor(out=xs(m), in0=ps, scalar=mcol(l, 2, m), in1=xs(m),
                                                                 op0=ALU.mult, op1=ALU.add), reads=[pb, modb[l]], writes=[xb[m]])
        if KS < 7:
            return
        norm_mod(l, 3, 4)
        for J in range(HC):
            accs = []
            for half in range(2):
                ch = half * HC + J
                ps, pb = proj(S_up[l], ch, hfn, hb)
                zb, zbb = tmp()
                co = (l * 2 * HC + ch) * 2
                P.op(POOL, lambda e, zb=zb, co=co: e.tensor_copy(out=zb[:, 0:2], in_=carF[:, co:co + 2]), reads=[carFb], writes=[zbb])
                P.op(ACT, lambda e, zb=zb, ps=ps: e.activation(out=zb[:, 2:2 + T], in_=ps, func=AF.Copy), reads=[pb], writes=[zbb])
                P.op(POOL, lambda e, zb=zb, co=co: e.tensor_copy(out=carF[:, co:co + 2], in_=zb[:, T:T + 2]), reads=[zbb], writes=[carFb])
                ac, acb = tmp()
                P.op(DVE, lambda e, ac=ac, zb=zb, ch=ch: e.tensor_scalar(out=ac[:, 0:T], in0=zb[:, 0:T], scalar1=pcol(l, "fconv", ch),
                                                                      scalar2=None, op0=ALU.mult), reads=[zbb, parb], writes=[acb])
                for k in (1, 2):
                    P.op(DVE, lambda e, ac=ac, zb=zb, ch=ch, k=k: e.scalar_tensor_tensor(
                        out=ac[:, 0:T], in0=zb[:, k:k + T], scalar=pcol(l, "fconv", k * 2 * HC + ch), in1=ac[:, 0:T],
                        op0=ALU.mult, op1=ALU.add), reads=[zbb, parb], writes=[acb])
                accs.append((ac, acb))
            (ag, agb), (av, avb) = accs
            P.op(ACT, lambda e, ag=ag: e.activation(out=ag[:, 0:T], in_=ag[:, 0:T], func=AF.Silu), writes=[agb])
            P.op(DVE, lambda e, ag=ag, av=av, J=J: e.tensor_tensor(out=ra(J), in0=ag[:, 0:T], in1=av[:, 0:T], op=ALU.mult),
                 reads=[agb, avb], writes=[rab[J]])
        for m in range(DC):
            ps, pb = psum()
            ngp = len(cfg.KG)
            for g, (k0, n) in enumerate(cfg.KG):
                sl, slb = load_slab(S_dn[l][g], m)
                fns = [(lambda e, k=k, sl=sl, k0=k0, g=g, n=n, ps=ps: e.matmul(
                    ps, lhsT=sl[:, k * 128:(k + 1) * 128], rhs=ra(k0 + k),
                    start=(g == 0 and k == 0), stop=(g == ngp - 1 and k == n - 1))) for k in range(n)]
                P.group(PE, fns, reads=[slb] + rab[k0:k0 + n], writes=[pb])
            P.op(DVE, lambda e, ps=ps, m=m: e.scalar_tensor_tensor(out=xs(m), in0=ps, scalar=mcol(l, 5, m), in1=xs(m),
                                                                 op0=ALU.mult, op1=ALU.add), reads=[pb, modb[l]], writes=[xb[m]])

    for ti in range(NT):
        if "nomain" in DBG:
            break
        t0 = ti * T
        P.dma(ACT, x_sb[:, :].rearrange("p (k t) -> p k t", t=T),
              xT[:, t0:t0 + T].rearrange("(k p) t -> p k t", p=128), writes=xb)
        for l in range(L):
            layer(l, ti)
        rstd, rsb = rms_to_h(None, None)
        ofg = L * cfg.NPL
        for kc in range(DC):
            P.op(DVE, lambda e, kc=kc: e.tensor_tensor(out=xs(kc), in0=xs(kc), in1=rstd[:, 0:T], op=ALU.mult),
                 reads=[rsb], writes=[xb[kc]])
            P.op(ACT, lambda e, kc=kc: e.activation(out=xs(kc), in_=xs(kc), func=AF.Copy, scale=par[:, ofg + kc:ofg + kc + 1]),
                 reads=[parb], writes=[xb[kc]])
        P.dma(ACT, outT[:, t0:t0 + T].rearrange("(k p) t -> p k t", p=128),
              x_sb[:, :].rearrange("p (k t) -> p k t", t=T), reads=xb, is_out=True)
    for tok in P.out_toks:
        ACT.ops.append(("wait", tok[0], tok[1]))

    with nc.Block() as block:
        @block.tensor
        def _(h):
            P.emit(PE, h)

        @block.scalar
        def _(h):
            P.emit(ACT, h)

        @block.vector
        def _(h):
            P.emit(DVE, h)

        @block.gpsimd
        def _(h):
            P.emit(POOL, h)

        @block.sync
        def _(h):
            P.emit(SP, h)
    es.close()
    return nc


def _pp(v):
    v = np.asarray(v, np.float32)
    return np.ascontiguousarray(v.reshape(-1, 128).T)


def prep_inputs(cfg, inp):
    D, S, HID, L = cfg.D, cfg.S, cfg.HID, cfg.L
    W, WC, DC, HC = cfg.W, cfg.WC, cfg.DC, cfg.HC
    f = lambda a: np.ascontiguousarray(np.asarray(a, np.float32))
    cols = []
    for l in range(L):
        cols.append(_pp(inp["norm_mix_g"][l]))
        cols.append(_pp(inp["ada_b"][l]))
        cols.append(_pp(inp["pool_scale"][l]))
        cols.append(np.concatenate([_pp(inp["sconv_w"][l][k]) for k in range(3)], axis=1))
        cols.append(np.concatenate([_pp(inp["conf_dw_w"][l][k]) for k in range(CK)], axis=1))
        cols.append(_pp(inp["conf_dw_b"][l]))
        cols.append(_pp(inp["conf_ln_g"][l]))
        cols.append(_pp(inp["conf_ln_b"][l]))
        cols.append(_pp(inp["gate_b"][l]))
        cols.append(_pp(inp["norm_ffn_g"][l]))
        cols.append(np.concatenate([_pp(inp["ffn_conv"][l][k]) for k in range(3)], axis=1))
    cols.append(_pp(inp["final_g"]))
    params = np.ascontiguousarray(np.concatenate(cols, axis=1))
    assert params.shape == (128, cfg.NP), params.shape
    bc = np.empty((L, 128, cfg.NBC), np.float32)
    for l in range(L):
        bc[l, :, 0:W] = np.asarray(inp["sgu_ln_g"][l])[None, :]
        bc[l, :, W:2 * W] = np.asarray(inp["sgu_ln_b"][l])[None, :]
        bc[l, :, 2 * W:3 * W] = np.asarray(inp["sgu_b"][l]).reshape(1, W)
    consts = np.zeros((128, 320), np.float32)
    for g, w in enumerate(WINS):
        consts[:, g * 16:(g + 1) * 16] = (1.0 / np.minimum(np.arange(16) + 1, w))[None, :]
    consts[:, 64:192] = np.triu(np.ones((128, 128), np.float32))
    consts[:, 192:320] = 1.0
    sguT = np.ascontiguousarray(np.transpose(np.asarray(inp["sgu_w"], np.float32), (0, 1, 3, 2))).reshape(L * cfg.HEADS * 128, 128)
    shared = {
        "ada_w": f(inp["ada_w"]).reshape(L * D, 6 * D),
        "w_in": f(inp["w_in"]).reshape(L * D, cfg.NIN),
        "w_branch": f(inp["w_branch"]).reshape(L * D, D),
        "w_out": f(inp["w_out"]).reshape(L * D, D),
        "ffn_up": f(inp["ffn_up"]).reshape(L * D, 2 * HID),
        "ffn_down": f(inp["ffn_down"]).reshape(L * HID, D),
        "pool_w": f(inp["pool_w"]).reshape(L * 4 * cfg.GD, cfg.GD),
        "params": params,
        "bcast": bc.reshape(L * 128, cfg.NBC),
        "consts": consts,
        "sguT": sguT,
    }
    x = np.asarray(inp["x"], np.float32)
    c = np.asarray(inp["c"], np.float32)
    per = []
    for b in range(x.shape[0]):
        per.append({"xT": np.ascontiguousarray(x[b].T), "cT": _pp(c[b])})
    return shared, per


def run_cfg(cfg, inp):
    shared, per = prep_inputs(cfg, inp)
    nb = len(per)
    nc = build_nc(cfg)
    in_maps = [dict(shared, **per[b]) for b in range(nb)]
    res = run_bass_kernel_spmd(nc, in_maps, core_ids=list(range(nb)))
    out = np.stack([np.ascontiguousarray(res.results[b]["outT"].T) for b in range(nb)], axis=0)
    return out.astype(np.float32)


def kernel(**inputs):
    cfg = Cfg()
    return run_cfg(cfg, inputs)
```

```python
import numpy as np
import concourse.bass as bass
import concourse.mybir as mybir
from concourse.bass_utils import run_bass_kernel_spmd

F32 = mybir.dt.float32
BF16 = mybir.dt.bfloat16
AF = mybir.ActivationFunctionType
ALU = mybir.AluOpType

RMS_EPS = 1e-6
LN_EPS = 1e-5
WINS = (2, 4, 8, 16)
CK = 31


class Cfg:
    def __init__(self, D=4096, S=2048, HID=11008, L=2, T=256):
        self.D, self.S, self.HID, self.L, self.T = D, S, HID, L, T
        self.DC = D // 128
        self.W = D // 4
        self.WC = self.W // 128
        self.GD = self.W // 4
        self.GC = self.GD // 128
        self.HC = HID // 128
        self.NIN = 8 * self.W + 4 * D
        self.NT = S // T
        self.HEADS = self.W // 128
        assert self.GD % 128 == 0 and HID % 256 == 0 and S % T == 0 and T % 128 == 0
        o = 0
        self.po = {}
        for name, n in (("nmg", self.DC), ("adab", 6 * self.DC), ("pscale", self.WC),
                        ("sconv", 3 * self.WC), ("cdw", CK * self.WC), ("cdb", self.WC),
                        ("clg", self.WC), ("clb", self.WC), ("gateb", 4 * self.DC),
                        ("nfg", self.DC), ("fconv", 3 * 2 * self.HC)):
            self.po[name] = o
            o += n
        self.NPL = o
        self.NP = o * L + self.DC
        self.KG = []
        k = 0
        while k < self.HC:
            n = min(32, self.HC - k)
            self.KG.append((k, n))
            k += n
        self.NBC = 3 * self.W


class Buf:
    __slots__ = ("w", "r")

    def __init__(self):
        self.w = None
        self.r = {}


class Eng:
    def __init__(self, name, sem, is_pe=False):
        self.name, self.sem, self.is_pe = name, sem, is_pe
        self.count = 0
        self.known = {}
        self.ops = []


class Prog:
    NDS = 24

    def __init__(self, nc, sems, dsems):
        self.nc = nc
        self.pe = Eng("pe", sems[0], True)
        self.act = Eng("act", sems[1])
        self.dve = Eng("dve", sems[2])
        self.pool = Eng("pool", sems[3])
        self.sp = Eng("sp", None)
        self.dsems = dsems
        self.dvals = [0] * len(dsems)
        self.dma_i = 0
        self.out_toks = []

    def _deps(self, e, reads, writes, extra=()):
        deps = {}

        def add(tok):
            if tok is None:
                return
            k = id(tok[0])
            if k not in deps or deps[k][1] < tok[1]:
                deps[k] = tok
        for b in reads:
            add(b.w)
        for b in writes:
            add(b.w)
            for t in b.r.values():
                add(t)
        for t in extra:
            add(t)
        for k, (sem, val) in deps.items():
            if e.is_pe and sem is e.sem:
                continue
            if e.known.get(k, 0) >= val:
                continue
            e.known[k] = val
            e.ops.append(("wait", sem, val))

    def _mark(self, tok, reads, writes):
        for b in writes:
            b.w = tok
            b.r = {}
        k = id(tok[0])
        for b in reads:
            if b.w is not tok:
                b.r[k] = tok

    def op(self, e, fn, reads=(), writes=()):
        self._deps(e, reads, writes)
        e.count += 1
        tok = (e.sem, e.count)
        e.ops.append(("op", fn, True))
        self._mark(tok, reads, writes)

    def group(self, e, fns, reads=(), writes=()):
        self._deps(e, reads, writes)
        e.count += 1
        tok = (e.sem, e.count)
        n = len(fns)
        for i, fn in enumerate(fns):
            e.ops.append(("op", fn, i == n - 1))
        self._mark(tok, reads, writes)

    def dma(self, q, out_ap, in_ap, reads=(), writes=(), is_out=False):
        i = self.dma_i % len(self.dsems)
        self.dma_i += 1
        s = self.dsems[i]
        prev = self.dvals[i]
        self.dvals[i] += 16
        val = self.dvals[i]
        extra = [(s, prev)] if prev > 0 else []
        self._deps(q, reads, writes, extra)
        q.ops.append(("dma", out_ap, in_ap, s))
        tok = (s, val)
        self._mark(tok, reads, writes)
        if is_out:
            self.out_toks.append(tok)

    def emit(self, e, h):
        for o in e.ops:
            if o[0] == "wait":
                h.wait_ge(o[1], o[2])
            elif o[0] == "op":
                ins = o[1](h)
                if o[2]:
                    ins.then_inc(e.sem, 1)
            else:
                h.dma_start(out=o[1], in_=o[2]).then_inc(o[3], 16)


def build_nc(cfg):
    D, S, HID, L, T = cfg.D, cfg.S, cfg.HID, cfg.L, cfg.T
    DC, W, WC, GD, GC, HC, NIN, NT, HEADS = (cfg.DC, cfg.W, cfg.WC, cfg.GD, cfg.GC, cfg.HC,
                                             cfg.NIN, cfg.NT, cfg.HEADS)
    nc = bass.Bass("TRN2", target_bir_lowering=False)

    def din(name, shape, dt=F32):
        return nc.dram_tensor(name, list(shape), dt, kind="ExternalInput").ap()

    xT = din("xT", [D, S])
    cT = din("cT", [128, DC])
    ada_w = din("ada_w", [L * D, 6 * D])
    w_in = din("w_in", [L * D, NIN])
    w_branch = din("w_branch", [L * D, D])
    w_out = din("w_out", [L * D, D])
    ffn_up = din("ffn_up", [L * D, 2 * HID])
    ffn_down = din("ffn_down", [L * HID, D])
    pool_w = din("pool_w", [L * 4 * GD, GD])
    params_d = din("params", [128, cfg.NP])
    bcast_d = din("bcast", [L * 128, cfg.NBC])
    consts_d = din("consts", [128, 64 + 128 + 128])
    sguT = din("sguT", [L * HEADS * 128, 128])
    outT = nc.dram_tensor("outT", [D, S], F32, kind="ExternalOutput").ap()

    def dint(name, n, kc):
        return nc.dram_tensor(name, [n, 128, kc * 128], BF16, kind="Internal").ap()

    class Slabs:
        def __init__(self, name, n, kc, ngroups):
            self.ap = dint(name, n, kc)
            self.kc = kc
            self.bufs = [[Buf() for _ in range(ngroups)] for _ in range(n)]

    def ng(kc):
        return (kc + 7) // 8

    S_in = [Slabs(f"s_in{l}", NIN // 128, DC, ng(DC)) for l in range(L)]
    S_br = [Slabs(f"s_br{l}", 4 * DC, WC, ng(WC)) for l in range(L)]
    S_out = [Slabs(f"s_out{l}", DC, DC, ng(DC)) for l in range(L)]
    S_up = [Slabs(f"s_up{l}", 2 * HC, DC, ng(DC)) for l in range(L)]
    S_dn = [[Slabs(f"s_dn{l}_{g}", DC, n, ng(n)) for g, (k0, n) in enumerate(cfg.KG)]
            for l in range(L)]
    S_pw = [Slabs(f"s_pw{l}", 4 * GC, GC, ng(GC)) for l in range(L)]

    from contextlib import ExitStack
    es = ExitStack()

    def sb(name, shape, dt):
        return es.enter_context(nc.sbuf_tensor(name, list(shape), dt))

    TW = T + 32
    NTMP = 14
    NSLOT = 4
    x_sb = sb("x_sb", [128, DC * T], F32)
    h_sb = sb("h_sb", [128, DC * T], BF16)
    RAU = max(HC, 4 * WC + DC, 80)
    ra32 = sb("ra", [128, RAU * T // 2], F32)
    ra16 = ra32[:, :].bitcast(BF16)
    slots = [sb(f"slot{i}", [128, 32 * 128], BF16) for i in range(NSLOT)]
    scr = sb("scr", [128, max(2 * W, WC * T, 2048)], F32)
    NSCR = max(2 * W, WC * T, 2048) // T
    vT = sb("vT", [128, 2 * W], BF16)
    ydt = sb("ydt", [128, WC * T], F32)
    bc = sb("bc", [128, cfg.NBC], F32)
    cst = sb("cst", [128, 64 + 128 + 128], F32)
    par = sb("par", [128, cfg.NP], F32)
    modd = sb("modd", [128, L * 6 * DC], F32)
    tmps = [sb(f"tmp{i}", [128, TW], F32) for i in range(NTMP)]
    pl = [sb(f"pl{i}", [128, T], BF16) for i in range(2 * GC)]
    wm = sb("wm", [128, L * HEADS * 128], BF16)
    ones32 = sb("ones32", [128, 128], F32)
    condT = sb("condT", [128, DC], F32)
    rowsb = [sb(f"rowsb{i}", [1, 256], F32) for i in range(2)]
    smalls = sb("smalls", [128, 64], F32)
    rstd_t = sb("rstd_t", [128, T], F32)
    carA = sb("carA", [128, L * WC * 16], F32)
    carC = sb("carC", [128, L * WC * 2], F32)
    carD = sb("carD", [128, L * WC * 30], F32)
    carF = sb("carF", [128, L * 2 * HC * 2], F32)
    pst = [es.enter_context(nc.psum_tensor(f"ps{i}", [128, 512], F32)) for i in range(8)]

    sems = [es.enter_context(nc.semaphore(f"se{i}")) for i in range(4)]
    dsems = [es.enter_context(nc.semaphore(f"sd{i}")) for i in range(Prog.NDS)]
    P = Prog(nc, sems, dsems)
    PE, ACT, DVE, POOL, SP = P.pe, P.act, P.dve, P.pool, P.sp

    xb = [Buf() for _ in range(DC)]
    hb = [Buf() for _ in range(DC)]
    rab = [Buf() for _ in range(RAU)]
    slotb = [Buf() for _ in range(NSLOT)]
    scrb = [Buf() for _ in range(NSCR)]
    vTb = [Buf() for _ in range(2)]
    ydb = [Buf() for _ in range(WC)]
    bcb, cstb, parb, wmb, onesb, condb, smallb = Buf(), Buf(), Buf(), Buf(), Buf(), Buf(), Buf()
    modb = [Buf() for _ in range(L)]
    tmpb = [Buf() for _ in range(NTMP)]
    plb = [Buf() for _ in range(2 * GC)]
    rowb = [Buf(), Buf()]
    carAb, carCb, carDb, carFb = Buf(), Buf(), Buf(), Buf()
    rstdb = Buf()
    psb = [Buf() for _ in range(16)]
    st = {"ps": 0, "tmp": 0, "slot": 0, "cast": 0}

    def psum(reserved=None):
        if reserved is None:
            i = st["ps"] % 7
            st["ps"] += 1
        else:
            i = 7
        return pst[i][:, 0:256], psb[i]

    def tmp():
        i = st["tmp"] % NTMP
        st["tmp"] += 1
        return tmps[i], tmpb[i]

    def xs(kc):
        return x_sb[:, kc * T:(kc + 1) * T]

    def hs(kc):
        return h_sb[:, kc * T:(kc + 1) * T]

    def ra(u):
        return ra16[:, u * T:(u + 1) * T]

    def pcol(l, name, j):
        o = l * cfg.NPL + cfg.po[name] + j
        return par[:, o:o + 1]

    def mcol(l, k, j):
        o = (l * 6 + k) * DC + j
        return modd[:, o:o + 1]

    NST32, NST16 = 4, 4
    st32 = [ra32[:, i * 2048:(i + 1) * 2048] for i in range(4)]
    st32b = [rab[i * 16:(i + 1) * 16] for i in range(4)]
    scr16 = scr[:, :].bitcast(BF16)
    st16 = [ra16[:, 16384 + i * 2048: 16384 + (i + 1) * 2048] for i in range(2)] + \
           [scr16[:, i * 2048:(i + 1) * 2048] for i in range(2)]
    u16 = 4096 // (T * 4)
    st16b = [rab[64 + i * 8: 64 + (i + 1) * 8] for i in range(2)] + [scrb[i * u16:(i + 1) * u16] for i in range(2)]
    assert RAU >= 80 and NSCR * T * 4 >= 8192

    P.dma(ACT, par[:, :], params_d, writes=[parb])
    P.dma(ACT, cst[:, :], consts_d, writes=[cstb])
    P.dma(ACT, condT[:, :], cT, writes=[condb])
    P.op(ACT, lambda e: e.activation(out=condT[:, :], in_=condT[:, :], func=AF.Silu), writes=[condb])
    P.op(DVE, lambda e: e.tensor_copy(out=ones32[:, :], in_=cst[:, 192:320]), reads=[cstb], writes=[onesb])
    for cb, ct in ((carAb, carA), (carCb, carC), (carDb, carD), (carFb, carF)):
        P.op(POOL, lambda e, ct=ct: e.memset(ct[:, :], 0.0), writes=[cb])

    stc = {"i": 0, "o": 0}

    def stage_in(src_ap, nk, ncol):
        i = stc["i"] % NST32
        stc["i"] += 1
        v = st32[i].rearrange("p (k c) -> p k c", c=256)[:, 0:nk, 0:ncol]
        P.dma(SP, v, src_ap.rearrange("(k p) c -> p k c", p=128), writes=st32b[i])
        return i, v

    cast_engs = [DVE, POOL, ACT]

    def cast(out_ap, in_ap, reads, writes):
        e = cast_engs[st["cast"] % 3]
        st["cast"] += 1
        if e is ACT:
            P.op(e, lambda g: g.activation(out=out_ap, in_=in_ap, func=AF.Copy), reads=reads, writes=writes)
        else:
            P.op(e, lambda g: g.tensor_copy(out=out_ap, in_=in_ap), reads=reads, writes=writes)

    pending = []
    LAG = 2

    def flush_stores(n_keep):
        while len(pending) > n_keep:
            a = pending.pop(0)
            P.dma(SP, a[0], a[1], reads=a[2], writes=a[3])

    def precast(src, K, N, slabs, slab_base):
        KC = K // 128
        MC = N // 128
        m = 0
        oi = 0
        while m < MC:
            nm = 2 if m + 1 < MC else 1
            for g in range(ng(KC)):
                k0 = g * 8
                nk = min(8, KC - k0)
                i, v = stage_in(src[k0 * 128:(k0 + nk) * 128, m * 128:(m + nm) * 128], nk, nm * 128)
                j = stc["o"] % NST16
                stc["o"] += 1
                o16 = st16[j].rearrange("p (m k c) -> p m k c", m=2, c=128)
                for mm in range(nm):
                    cast(o16[:, mm, 0:nk, :], v[:, :, mm * 128:(mm + 1) * 128], st32b[i], st16b[j])
                dst = slabs.ap[slab_base + m:slab_base + m + nm, :, k0 * 128:(k0 + nk) * 128]
                pending.append((dst.rearrange("m p (k c) -> p m k c", c=128), o16[:, 0:nm, 0:nk, :],
                                st16b[j], [slabs.bufs[slab_base + m + mm][g] for mm in range(nm)]))
                flush_stores(LAG)
            m += nm

    def ada_layer(l):
        pT, pTb = psum(reserved=0)
        import os
        KA = int(os.environ.get("KADA", "9"))
        for s in range(6 * D // 256):
            prow, prb = psum()
            for g in range(ng(DC)):
                k0 = g * 8
                nk = min(8, DC - k0)
                i, v = stage_in(ada_w[l * D + k0 * 128: l * D + (k0 + nk) * 128, s * 256:(s + 1) * 256], nk, 256)
                fns = [(lambda e, kk=kk, v=v, prow=prow, k0=k0:
                        e.matmul(prow[0:1, :], lhsT=condT[:, k0 + kk:k0 + kk + 1], rhs=v[:, kk, :],
                                 start=(k0 + kk == 0), stop=(k0 + kk == DC - 1))) for kk in range(nk)]
                if KA >= 2:
                    P.group(PE, fns, reads=st32b[i] + [condb], writes=[prb])
            r = s % 2
            if KA < 3:
                continue
            P.op(ACT, lambda e, r=r, prow=prow: e.activation(out=rowsb[r][0:1, :], in_=prow[0:1, :], func=AF.Copy),
                 reads=[prb], writes=[rowb[r]])
            fns = [(lambda e, mm=mm, r=r, s=s:
                    e.matmul(pT[:, 2 * s + mm:2 * s + mm + 1], lhsT=rowsb[r][0:1, mm * 128:(mm + 1) * 128],
                             rhs=ones32[0:1, 0:1], start=True, stop=True)) for mm in range(2)]
            if KA >= 4:
                P.group(PE, fns, reads=[rowb[r], onesb], writes=[pTb])
        if KA < 5:
            return
        o = l * cfg.NPL + cfg.po["adab"]
        raw, rawb = tmp()
        P.op(DVE, lambda e: e.tensor_tensor(out=raw[:, 0:6 * DC], in0=pT[:, 0:6 * DC], in1=par[:, o:o + 6 * DC], op=ALU.add),
             reads=[pTb, parb], writes=[rawb])
        base = l * 6 * DC
        og = l * cfg.NPL + cfg.po["nmg"]
        of = l * cfg.NPL + cfg.po["nfg"]
        P.op(DVE, lambda e: e.scalar_tensor_tensor(out=modd[:, base:base + DC], in0=raw[:, DC:2 * DC], scalar=1.0,
                                                   in1=par[:, og:og + DC], op0=ALU.add, op1=ALU.mult),
             reads=[rawb, parb], writes=[modb[l]])
        P.op(DVE, lambda e: e.tensor_copy(out=modd[:, base + DC:base + 2 * DC], in_=raw[:, 0:DC]), reads=[rawb], writes=[modb[l]])
        P.op(DVE, lambda e: e.tensor_copy(out=modd[:, base + 2 * DC:base + 3 * DC], in_=raw[:, 2 * DC:3 * DC]), reads=[rawb], writes=[modb[l]])
        P.op(DVE, lambda e: e.scalar_tensor_tensor(out=modd[:, base + 3 * DC:base + 4 * DC], in0=raw[:, 4 * DC:5 * DC], scalar=1.0,
                                                   in1=par[:, of:of + DC], op0=ALU.add, op1=ALU.mult),
             reads=[rawb, parb], writes=[modb[l]])
        P.op(DVE, lambda e: e.tensor_copy(out=modd[:, base + 4 * DC:base + 5 * DC], in_=raw[:, 3 * DC:4 * DC]), reads=[rawb], writes=[modb[l]])
        P.op(DVE, lambda e: e.tensor_copy(out=modd[:, base + 5 * DC:base + 6 * DC], in_=raw[:, 5 * DC:6 * DC]), reads=[rawb], writes=[modb[l]])

    import os
    DBG = os.environ.get("KDBG", "")
    for l in range(L):
        if "noada" not in DBG:
            ada_layer(l)
        i, v = stage_in(sguT[l * HEADS * 128:(l + 1) * HEADS * 128, :], HEADS, 128)
        for hh in range(HEADS):
            o = (l * HEADS + hh) * 128
            P.op(DVE, lambda e, v=v, hh=hh, o=o: e.tensor_tensor(out=wm[:, o:o + 128], in0=v[:, hh, :], in1=cst[:, 64:192], op=ALU.mult),
                 reads=st32b[i] + [cstb], writes=[wmb])
    for l in range(L):
        if "nocast" in DBG:
            break
        precast(w_in[l * D:(l + 1) * D, :], D, NIN, S_in[l], 0)
        for g in range(4):
            precast(pool_w[(l * 4 + g) * GD:(l * 4 + g + 1) * GD, :], GD, GD, S_pw[l], g * GC)
        for i in range(4):
            precast(w_branch[l * D + i * W: l * D + (i + 1) * W, :], W, D, S_br[l], i * DC)
        precast(w_out[l * D:(l + 1) * D, :], D, D, S_out[l], 0)
        precast(ffn_up[l * D:(l + 1) * D, :], D, 2 * HID, S_up[l], 0)
        for g, (k0, n) in enumerate(cfg.KG):
            precast(ffn_down[l * HID + k0 * 128: l * HID + (k0 + n) * 128, :], n * 128, D, S_dn[l][g], 0)

    flush_stores(0)

    def load_slab(slabs, idx):
        i = st["slot"] % NSLOT
        st["slot"] += 1
        n = slabs.kc * 128
        P.dma(SP, slots[i][:, 0:n], slabs.ap[idx], reads=slabs.bufs[idx], writes=[slotb[i]])
        return slots[i], slotb[i]

    def proj(slabs, idx, rhs_fn, rhs_bufs, nk=None):
        sl, slb = load_slab(slabs, idx)
        nk = slabs.kc
        ps, pb = psum()
        fns = [(lambda e, k=k: e.matmul(ps, lhsT=sl[:, k * 128:(k + 1) * 128], rhs=rhs_fn(k),
                                        start=(k == 0), stop=(k == nk - 1))) for k in range(nk)]
        P.group(PE, fns, reads=[slb] + rhs_bufs, writes=[pb])
        return ps, pb

    def rsqrt_inplace(ap, buf):
        P.op(ACT, lambda e: e.activation(out=ap, in_=ap, func=AF.Sqrt), writes=[buf])
        P.op(DVE, lambda e: e.reciprocal(out=ap, in_=ap), writes=[buf])

    def rms_to_h(Acol, Bcol):
        ps, pb = psum()
        for kc in range(DC):
            sq, sqb = tmp()
            P.op(ACT, lambda e, kc=kc, sq=sq: e.activation(out=sq[:, 0:T], in_=xs(kc), func=AF.Square),
                 reads=[xb[kc]], writes=[sqb])
            P.group(PE, [lambda e, kc=kc, sq=sq: e.matmul(ps, lhsT=ones32[:, :], rhs=sq[:, 0:T],
                                                         start=(kc == 0), stop=(kc == DC - 1))],
                    reads=[sqb, onesb], writes=[pb])
        rstd, rsb = rstd_t, rstdb
        P.op(DVE, lambda e: e.tensor_scalar(out=rstd[:, 0:T], in0=ps, scalar1=1.0 / D, scalar2=RMS_EPS,
                                            op0=ALU.mult, op1=ALU.add), reads=[pb], writes=[rsb])
        rsqrt_inplace(rstd[:, 0:T], rsb)
        return rstd, rsb

    def norm_mod(l, ka, kb):
        rstd, rsb = rms_to_h(None, None)
        for kc in range(DC):
            t, tb = tmp()
            P.op(DVE, lambda e, kc=kc, t=t: e.tensor_tensor(out=t[:, 0:T], in0=xs(kc), in1=rstd[:, 0:T], op=ALU.mult),
                 reads=[xb[kc], rsb], writes=[tb])
            P.op(ACT, lambda e, kc=kc, t=t: e.activation(out=hs(kc), in_=t[:, 0:T], func=AF.Identity,
                                                         bias=mcol(l, kb, kc), scale=mcol(l, ka, kc)),
                 reads=[tb, modb[l]], writes=[hb[kc]])

    hfn = lambda k: hs(k)

    KS = int(os.environ.get("KSTG", "99"))

    def layer(l, ti):
        first = (ti == 0)
        P.dma(ACT, bc[:, :], bcast_d[l * 128:(l + 1) * 128, :], writes=[bcb])
        norm_mod(l, 0, 1)
        YA, YB, YC, YD = 0, WC, 2 * WC, 3 * WC
        MG = 4 * WC
        if KS < 4:
            return
        yu = 1
        for c in range(WC):
            ps_a, pab = proj(S_in[l], 6 * WC + c, hfn, hb)
            ps_g, pgb = proj(S_in[l], 7 * WC + c, hfn, hb)
            sg, sgb = tmp()
            P.op(ACT, lambda e, sg=sg, ps_g=ps_g: e.activation(out=sg[:, 0:T], in_=ps_g, func=AF.Sigmoid), reads=[pgb], writes=[sgb])
            yb, ybb = tmp()
            co = (l * WC + c) * 30
            P.op(POOL, lambda e, yb=yb, co=co: e.tensor_copy(out=yb[:, 0:30], in_=carD[:, co:co + 30]), reads=[carDb], writes=[ybb])
            P.op(DVE, lambda e, yb=yb, sg=sg, ps_a=ps_a: e.tensor_tensor(out=yb[:, 30:30 + T], in0=sg[:, 0:T], in1=ps_a, op=ALU.mult),
                 reads=[sgb, pab], writes=[ybb])
            P.op(POOL, lambda e, yb=yb, co=co: e.tensor_copy(out=carD[:, co:co + 30], in_=yb[:, T:T + 30]), reads=[ybb], writes=[carDb])
            yd = ydt[:, c * T:(c + 1) * T]
            P.op(DVE, lambda e, yd=yd, yb=yb, c=c: e.tensor_scalar(out=yd, in0=yb[:, 0:T], scalar1=pcol(l, "cdw", c),
                                                                 scalar2=pcol(l, "cdb", c), op0=ALU.mult, op1=ALU.add),
                 reads=[ybb, parb], writes=[ydb[c]])
            for k in range(1, CK):
                P.op(DVE, lambda e, yd=yd, yb=yb, c=c, k=k: e.scalar_tensor_tensor(
                    out=yd, in0=yb[:, k:k + T], scalar=pcol(l, "cdw", k * WC + c), in1=yd, op0=ALU.mult, op1=ALU.add),
                    reads=[ybb, parb], writes=[ydb[c]])
        ps_s, pssb = psum()
        ps_q, psqb = psum()
        for c in range(WC):
            yd = ydt[:, c * T:(c + 1) * T]
            P.group(PE, [lambda e, yd=yd, c=c: e.matmul(ps_s, lhsT=ones32[:, :], rhs=yd, start=(c == 0), stop=(c == WC - 1))],
                    reads=[ydb[c], onesb], writes=[pssb])
            sq, sqb = tmp()
            P.op(ACT, lambda e, sq=sq, yd=yd: e.activation(out=sq[:, 0:T], in_=yd, func=AF.Square), reads=[ydb[c]], writes=[sqb])
            P.group(PE, [lambda e, sq=sq, c=c: e.matmul(ps_q, lhsT=ones32[:, :], rhs=sq[:, 0:T], start=(c == 0), stop=(c == WC - 1))],
                    reads=[sqb, onesb], writes=[psqb])
        mu, mub = tmp()
        rs, rsb = tmp()
        P.op(DVE, lambda e: e.tensor_scalar(out=mu[:, 0:T], in0=ps_s, scalar1=1.0 / W, scalar2=None, op0=ALU.mult), reads=[pssb], writes=[mub])
        P.op(DVE, lambda e: e.tensor_tensor(out=rs[:, 0:T], in0=mu[:, 0:T], in1=mu[:, 0:T], op=ALU.mult), reads=[mub], writes=[rsb])
        P.op(DVE, lambda e: e.scalar_tensor_tensor(out=rs[:, 0:T], in0=ps_q, scalar=1.0 / W, in1=rs[:, 0:T], op0=ALU.mult, op1=ALU.subtract),
             reads=[psqb], writes=[rsb])
        P.op(DVE, lambda e: e.tensor_scalar(out=rs[:, 0:T], in0=rs[:, 0:T], scalar1=LN_EPS, scalar2=None, op0=ALU.add), writes=[rsb])
        rsqrt_inplace(rs[:, 0:T], rsb)
        for c in range(WC):
            yd = ydt[:, c * T:(c + 1) * T]
            P.op(DVE, lambda e, yd=yd: e.tensor_tensor(out=yd, in0=yd, in1=mu[:, 0:T], op=ALU.subtract), reads=[mub], writes=[ydb[c]])
            P.op(DVE, lambda e, yd=yd: e.tensor_tensor(out=yd, in0=yd, in1=rs[:, 0:T], op=ALU.mult), reads=[rsb], writes=[ydb[c]])
            P.op(ACT, lambda e, yd=yd, c=c: e.activation(out=ra(YD + c), in_=yd, func=AF.Silu, bias=pcol(l, "clb", c), scale=pcol(l, "clg", c)),
                 reads=[ydb[c], parb], writes=[rab[YD + c]])
        if KS < 1:
            return
        for g in range(4):
            w = WINS[g]
            for c2 in range(GC):
                c = g * GC + c2
                ps, pb = proj(S_in[l], c, hfn, hb)
                KB = int(os.environ.get("KA2", "9"))
                if KB < 2:
                    continue
                ab, abb = tmp()
                co = (l * WC + c) * 16
                P.op(POOL, lambda e, ab=ab, co=co: e.tensor_copy(out=ab[:, 0:16], in_=carA[:, co:co + 16]), reads=[carAb], writes=[abb])
                P.op(ACT, lambda e, ab=ab, ps=ps: e.activation(out=ab[:, 16:16 + T], in_=ps, func=AF.Copy), reads=[pb], writes=[abb])
                P.op(POOL, lambda e, ab=ab, co=co: e.tensor_copy(out=carA[:, co:co + 16], in_=ab[:, T:T + 16]), reads=[abb], writes=[carAb])
                if KB < 3:
                    continue
                WB = 16 + T
                cur, curb = ab, abb
                lag = 1
                while lag < w:
                    nx, nxb = tmp()
                    lo = 2 * lag - 1
                    P.op(DVE, lambda e, nx=nx, cur=cur, lag=lag, lo=lo: e.tensor_tensor(
                        out=nx[:, lo:WB], in0=cur[:, lo:WB], in1=cur[:, lo - lag:WB - lag], op=ALU.add),
                        reads=[curb], writes=[nxb])
                    cur, curb = nx, nxb
                    lag *= 2
                pi = c2
                if KB < 4:
                    continue
                P.op(DVE, lambda e, cur=cur, ab=ab, pi=pi, w=w: e.scalar_tensor_tensor(
                    out=pl[pi][:, :], in0=cur[:, 16:16 + T], scalar=1.0 / w, in1=ab[:, 16:16 + T],
                    op0=ALU.mult, op1=ALU.subtract), reads=[curb, abb], writes=[plb[pi]])
                if first and KB >= 5:
                    t16, t16b = tmp()
                    P.op(DVE, lambda e, t16=t16, cur=cur, g=g: e.tensor_tensor(
                        out=t16[:, 0:16], in0=cur[:, 16:32], in1=cst[:, g * 16:(g + 1) * 16], op=ALU.mult),
                        reads=[curb, cstb], writes=[t16b])
                    P.op(DVE, lambda e, t16=t16, ab=ab, pi=pi: e.tensor_tensor(
                        out=pl[pi][:, 0:16], in0=t16[:, 0:16], in1=ab[:, 16:32], op=ALU.subtract),
                        reads=[t16b, abb], writes=[plb[pi]])
            for m2 in range(GC):
                if int(os.environ.get("KA2", "9")) < 6:
                    continue
                c = g * GC + m2
                ps, pb = proj(S_pw[l], g * GC + m2, lambda k: pl[k][:, :], plb[0:GC])
                P.op(ACT, lambda e, ps=ps, c=c: e.activation(out=ra(YA + c), in_=ps, func=AF.Copy, scale=pcol(l, "pscale", c)),
                     reads=[pb, parb], writes=[rab[YA + c]])
        if KS < 2:
            return
        for c in range(WC):
            ps, pb = proj(S_in[l], WC + c, hfn, hb)
            P.op(ACT, lambda e, ps=ps, c=c: e.activation(out=ra(YB + c), in_=ps, func=AF.Gelu), reads=[pb], writes=[rab[YB + c]])
        NS2 = T // 128
        su = W // T if W >= T else 1
        for j in range(WC):
            sl, slb = load_slab(S_in[l], 2 * WC + j)
            ps, pb = psum()
            fns = []
            for s2 in range(NS2):
                for k in range(DC):
                    fns.append(lambda e, s2=s2, k=k, sl=sl, ps=ps: e.matmul(
                        ps[:, s2 * 128:(s2 + 1) * 128], lhsT=h_sb[:, k * T + s2 * 128:k * T + (s2 + 1) * 128],
                        rhs=sl[:, k * 128:(k + 1) * 128], start=(k == 0), stop=(k == DC - 1)))
            P.group(PE, fns, reads=[slb] + hb, writes=[pb])
            for s2 in range(NS2):
                P.op(ACT, lambda e, s2=s2, j=j, ps=ps: e.activation(
                    out=scr[:, s2 * W + j * 128:s2 * W + (j + 1) * 128], in_=ps[:, s2 * 128:(s2 + 1) * 128], func=AF.Gelu),
                    reads=[pb], writes=scrb[s2 * su:(s2 + 1) * su])
        for s2 in range(NS2):
            vb = scr[:, s2 * W:(s2 + 1) * W]
            vbb = scrb[s2 * su:(s2 + 1) * su]
            nst = (W + 511) // 512
            so = 0
            for q in range(nst):
                a0, a1 = q * 512, min(W, (q + 1) * 512)
                P.op(DVE, lambda e, q=q, a0=a0, a1=a1, vb=vb: e.bn_stats(out=smalls[:, q * 6:(q + 1) * 6], in_=vb[:, a0:a1]),
                     reads=vbb, writes=[smallb])
            P.op(DVE, lambda e: e.bn_aggr(out=smalls[:, 32:34], in_=smalls[:, 0:nst * 6]), writes=[smallb])
            P.op(DVE, lambda e: e.tensor_scalar(out=smalls[:, 34:35], in0=smalls[:, 33:34], scalar1=LN_EPS, scalar2=None,
                                                op0=ALU.add), writes=[smallb])
            rsqrt_inplace(smalls[:, 34:35], smallb)
            P.op(DVE, lambda e, vb=vb: e.tensor_scalar(out=vb, in0=vb, scalar1=smalls[:, 32:33], scalar2=smalls[:, 34:35],
                                                       op0=ALU.subtract, op1=ALU.mult), reads=[smallb], writes=vbb)
            P.op(DVE, lambda e, vb=vb: e.tensor_tensor(out=vb, in0=vb, in1=bc[:, 0:W], op=ALU.mult), reads=[bcb], writes=vbb)
            P.op(DVE, lambda e, vb=vb, s2=s2: e.tensor_tensor(out=vT[:, s2 * W:(s2 + 1) * W], in0=vb, in1=bc[:, W:2 * W], op=ALU.add),
                 reads=[bcb] + vbb, writes=[vTb[s2]])
        for hh in range(HEADS):
            ps, pb = psum()
            o = (l * HEADS + hh) * 128
            fns = [lambda e, s2=s2, hh=hh, ps=ps, o=o: e.matmul(
                ps[:, s2 * 128:(s2 + 1) * 128], lhsT=vT[:, s2 * W + hh * 128:s2 * W + (hh + 1) * 128],
                rhs=wm[:, o:o + 128], start=True, stop=True) for s2 in range(NS2)]
            P.group(PE, fns, reads=[vTb[0], vTb[1], wmb], writes=[pb])
            t, tb = tmp()
            for s2 in range(NS2):
                P.op(DVE, lambda e, s2=s2, hh=hh, ps=ps, t=t: e.tensor_tensor(
                    out=t[:, s2 * 128:(s2 + 1) * 128], in0=ps[:, s2 * 128:(s2 + 1) * 128],
                    in1=bc[:, 2 * W + hh * 128:2 * W + (hh + 1) * 128], op=ALU.add), reads=[pb, bcb], writes=[tb])
            P.op(DVE, lambda e, hh=hh, t=t: e.tensor_tensor(out=ra(YB + hh), in0=t[:, 0:T], in1=ra(YB + hh), op=ALU.mult),
                 reads=[tb], writes=[rab[YB + hh]])
        if KS < 3:
            return
        for c in range(WC):
            ps_b, pbb = proj(S_in[l], 3 * WC + c, hfn, hb)
            ps_c, pcb = proj(S_in[l], 4 * WC + c, hfn, hb)
            ps_h, phb = proj(S_in[l], 5 * WC + c, hfn, hb)
            cg, cgb = tmp()
            P.op(ACT, lambda e, cg=cg, ps_c=ps_c: e.activation(out=cg[:, 0:T], in_=ps_c, func=AF.Copy), reads=[pcb], writes=[cgb])
            pr, prb = tmp()
            co = (l * WC + c) * 2
            P.op(POOL, lambda e, pr=pr, co=co: e.tensor_copy(out=pr[:, 0:2], in_=carC[:, co:co + 2]), reads=[carCb], writes=[prb])
            P.op(DVE, lambda e, pr=pr, cg=cg, ps_h=ps_h: e.tensor_tensor(out=pr[:, 2:2 + T], in0=cg[:, 0:T], in1=ps_h, op=ALU.mult),
                 reads=[cgb, phb], writes=[prb])
            P.op(POOL, lambda e, pr=pr, co=co: e.tensor_copy(out=carC[:, co:co + 2], in_=pr[:, T:T + 2]), reads=[prb], writes=[carCb])
            ac, acb = tmp()
            P.op(DVE, lambda e, ac=ac, pr=pr, c=c: e.tensor_scalar(out=ac[:, 0:T], in0=pr[:, 0:T], scalar1=pcol(l, "sconv", 0 * WC + c),
                                                                 scalar2=None, op0=ALU.mult), reads=[prb, parb], writes=[acb])
            for k in (1, 2):
                P.op(DVE, lambda e, ac=ac, pr=pr, c=c, k=k: e.scalar_tensor_tensor(
                    out=ac[:, 0:T], in0=pr[:, k:k + T], scalar=pcol(l, "sconv", k * WC + c), in1=ac[:, 0:T],
                    op0=ALU.mult, op1=ALU.add), reads=[prb, parb], writes=[acb])
            P.op(DVE, lambda e, ac=ac, ps_b=ps_b, c=c: e.tensor_tensor(out=ra(YC + c), in0=ac[:, 0:T], in1=ps_b, op=ALU.mult),
                 reads=[acb, pbb], writes=[rab[YC + c]])
        if KS < 5:
            return
        for m in range(DC):
            acc, accb = tmp()
            for i in range(4):
                ps_g, pgb = proj(S_in[l], 8 * WC + i * DC + m, hfn, hb)
                ps_b, pbb = proj(S_br[l], i * DC + m, lambda k, i=i: ra(i * WC + k), rab[i * WC:(i + 1) * WC])
                sg, sgb = tmp()
                P.op(ACT, lambda e, sg=sg, ps_g=ps_g, i=i, m=m: e.activation(out=sg[:, 0:T], in_=ps_g, func=AF.Sigmoid,
                                                                             bias=pcol(l, "gateb", i * DC + m)),
                     reads=[pgb, parb], writes=[sgb])
                if i == 0:
                    P.op(DVE, lambda e, acc=acc, sg=sg, ps_b=ps_b: e.tensor_tensor(out=acc[:, 0:T], in0=sg[:, 0:T], in1=ps_b, op=ALU.mult),
                         reads=[sgb, pbb], writes=[accb])
                else:
                    P.op(DVE, lambda e, sg=sg, ps_b=ps_b: e.tensor_tensor(out=sg[:, 0:T], in0=sg[:, 0:T], in1=ps_b, op=ALU.mult),
                         reads=[pbb], writes=[sgb])
                    if i < 3:
                        P.op(POOL, lambda e, acc=acc, sg=sg: e.tensor_tensor(out=acc[:, 0:T], in0=acc[:, 0:T], in1=sg[:, 0:T], op=ALU.add),
                             reads=[sgb], writes=[accb])
                    else:
                        P.op(POOL, lambda e, acc=acc, sg=sg, m=m: e.tensor_tensor(out=ra(MG + m), in0=acc[:, 0:T], in1=sg[:, 0:T], op=ALU.add),
                             reads=[sgb, accb], writes=[rab[MG + m]])
        if KS < 6:
            return
        for m in range(DC):
            ps, pb = proj(S_out[l], m, lambda k: ra(MG + k), rab[MG:MG + DC])
            P.op(DVE, lambda e, ps=ps, m=m: e.scalar_tensor_tensor(out=xs(m), in0=ps, scalar=mcol(l, 2, m), in1=xs(m),
                                                                 op0=ALU.mult, op1=ALU.add), reads=[pb, modb[l]], writes=[xb[m]])
        if KS < 7:
            return
        norm_mod(l, 3, 4)
        for J in range(HC):
            accs = []
            for half in range(2):
                ch = half * HC + J
                ps, pb = proj(S_up[l], ch, hfn, hb)
                zb, zbb = tmp()
                co = (l * 2 * HC + ch) * 2
                P.op(POOL, lambda e, zb=zb, co=co: e.tensor_copy(out=zb[:, 0:2], in_=carF[:, co:co + 2]), reads=[carFb], writes=[zbb])
                P.op(ACT, lambda e, zb=zb, ps=ps: e.activation(out=zb[:, 2:2 + T], in_=ps, func=AF.Copy), reads=[pb], writes=[zbb])
                P.op(POOL, lambda e, zb=zb, co=co: e.tensor_copy(out=carF[:, co:co + 2], in_=zb[:, T:T + 2]), reads=[zbb], writes=[carFb])
                ac, acb = tmp()
                P.op(DVE, lambda e, ac=ac, zb=zb, ch=ch: e.tensor_scalar(out=ac[:, 0:T], in0=zb[:, 0:T], scalar1=pcol(l, "fconv", ch),
                                                                      scalar2=None, op0=ALU.mult), reads=[zbb, parb], writes=[acb])
                for k in (1, 2):
                    P.op(DVE, lambda e, ac=ac, zb=zb, ch=ch, k=k: e.scalar_tensor_tensor(
                        out=ac[:, 0:T], in0=zb[:, k:k + T], scalar=pcol(l, "fconv", k * 2 * HC + ch), in1=ac[:, 0:T],
                        op0=ALU.mult, op1=ALU.add), reads=[zbb, parb], writes=[acb])
                accs.append((ac, acb))
            (ag, agb), (av, avb) = accs
            P.op(ACT, lambda e, ag=ag: e.activation(out=ag[:, 0:T], in_=ag[:, 0:T], func=AF.Silu), writes=[agb])
            P.op(DVE, lambda e, ag=ag, av=av, J=J: e.tensor_tensor(out=ra(J), in0=ag[:, 0:T], in1=av[:, 0:T], op=ALU.mult),
                 reads=[agb, avb], writes=[rab[J]])
        for m in range(DC):
            ps, pb = psum()
            ngp = len(cfg.KG)
            for g, (k0, n) in enumerate(cfg.KG):
                sl, slb = load_slab(S_dn[l][g], m)
                fns = [(lambda e, k=k, sl=sl, k0=k0, g=g, n=n, ps=ps: e.matmul(
                    ps, lhsT=sl[:, k * 128:(k + 1) * 128], rhs=ra(k0 + k),
                    start=(g == 0 and k == 0), stop=(g == ngp - 1 and k == n - 1))) for k in range(n)]
                P.group(PE, fns, reads=[slb] + rab[k0:k0 + n], writes=[pb])
            P.op(DVE, lambda e, ps=ps, m=m: e.scalar_tensor_tensor(out=xs(m), in0=ps, scalar=mcol(l, 5, m), in1=xs(m),
                                                                 op0=ALU.mult, op1=ALU.add), reads=[pb, modb[l]], writes=[xb[m]])

    for ti in range(NT):
        if "nomain" in DBG:
            break
        t0 = ti * T
        XG = 4 if DC % 4 == 0 else 1
        cg_ = DC // XG
        for q in range(XG):
            P.dma(ACT, x_sb[:, q * cg_ * T:(q + 1) * cg_ * T].rearrange("p (k t) -> p k t", t=T),
                  xT[q * cg_ * 128:(q + 1) * cg_ * 128, t0:t0 + T].rearrange("(k p) t -> p k t", p=128),
                  writes=xb[q * cg_:(q + 1) * cg_])
        for l in range(L):
            layer(l, ti)
        rstd, rsb = rms_to_h(None, None)
        ofg = L * cfg.NPL
        for kc in range(DC):
            P.op(DVE, lambda e, kc=kc: e.tensor_tensor(out=xs(kc), in0=xs(kc), in1=rstd[:, 0:T], op=ALU.mult),
                 reads=[rsb], writes=[xb[kc]])
            P.op(ACT, lambda e, kc=kc: e.activation(out=xs(kc), in_=xs(kc), func=AF.Copy, scale=par[:, ofg + kc:ofg + kc + 1]),
                 reads=[parb], writes=[xb[kc]])
        for q in range(XG):
            P.dma(ACT, outT[q * cg_ * 128:(q + 1) * cg_ * 128, t0:t0 + T].rearrange("(k p) t -> p k t", p=128),
                  x_sb[:, q * cg_ * T:(q + 1) * cg_ * T].rearrange("p (k t) -> p k t", t=T),
                  reads=xb[q * cg_:(q + 1) * cg_], is_out=True)
    for tok in P.out_toks:
        ACT.ops.append(("wait", tok[0], tok[1]))

    with nc.Block() as block:
        @block.tensor
        def _(h):
            P.emit(PE, h)

        @block.scalar
        def _(h):
            P.emit(ACT, h)

        @block.vector
        def _(h):
            P.emit(DVE, h)

        @block.gpsimd
        def _(h):
            P.emit(POOL, h)

        @block.sync
        def _(h):
            P.emit(SP, h)
    es.close()
    return nc


def _pp(v):
    v = np.asarray(v, np.float32)
    return np.ascontiguousarray(v.reshape(-1, 128).T)


def prep_inputs(cfg, inp):
    D, S, HID, L = cfg.D, cfg.S, cfg.HID, cfg.L
    W, WC, DC, HC = cfg.W, cfg.WC, cfg.DC, cfg.HC
    f = lambda a: np.ascontiguousarray(np.asarray(a, np.float32))
    cols = []
    for l in range(L):
        cols.append(_pp(inp["norm_mix_g"][l]))
        cols.append(_pp(inp["ada_b"][l]))
        cols.append(_pp(inp["pool_scale"][l]))
        cols.append(np.concatenate([_pp(inp["sconv_w"][l][k]) for k in range(3)], axis=1))
        cols.append(np.concatenate([_pp(inp["conf_dw_w"][l][k]) for k in range(CK)], axis=1))
        cols.append(_pp(inp["conf_dw_b"][l]))
        cols.append(_pp(inp["conf_ln_g"][l]))
        cols.append(_pp(inp["conf_ln_b"][l]))
        cols.append(_pp(inp["gate_b"][l]))
        cols.append(_pp(inp["norm_ffn_g"][l]))
        cols.append(np.concatenate([_pp(inp["ffn_conv"][l][k]) for k in range(3)], axis=1))
    cols.append(_pp(inp["final_g"]))
    params = np.ascontiguousarray(np.concatenate(cols, axis=1))
    assert params.shape == (128, cfg.NP), params.shape
    bc = np.empty((L, 128, cfg.NBC), np.float32)
    for l in range(L):
        bc[l, :, 0:W] = np.asarray(inp["sgu_ln_g"][l])[None, :]
        bc[l, :, W:2 * W] = np.asarray(inp["sgu_ln_b"][l])[None, :]
        bc[l, :, 2 * W:3 * W] = np.asarray(inp["sgu_b"][l]).reshape(1, W)
    consts = np.zeros((128, 320), np.float32)
    for g, w in enumerate(WINS):
        consts[:, g * 16:(g + 1) * 16] = (1.0 / np.minimum(np.arange(16) + 1, w))[None, :]
    consts[:, 64:192] = np.triu(np.ones((128, 128), np.float32))
    consts[:, 192:320] = 1.0
    sguT = np.ascontiguousarray(np.transpose(np.asarray(inp["sgu_w"], np.float32), (0, 1, 3, 2))).reshape(L * cfg.HEADS * 128, 128)
    shared = {
        "ada_w": f(inp["ada_w"]).reshape(L * D, 6 * D),
        "w_in": f(inp["w_in"]).reshape(L * D, cfg.NIN),
        "w_branch": f(inp["w_branch"]).reshape(L * D, D),
        "w_out": f(inp["w_out"]).reshape(L * D, D),
        "ffn_up": f(inp["ffn_up"]).reshape(L * D, 2 * HID),
        "ffn_down": f(inp["ffn_down"]).reshape(L * HID, D),
        "pool_w": f(inp["pool_w"]).reshape(L * 4 * cfg.GD, cfg.GD),
        "params": params,
        "bcast": bc.reshape(L * 128, cfg.NBC),
        "consts": consts,
        "sguT": sguT,
    }
    x = np.asarray(inp["x"], np.float32)
    c = np.asarray(inp["c"], np.float32)
    per = []
    for b in range(x.shape[0]):
        per.append({"xT": np.ascontiguousarray(x[b].T), "cT": _pp(c[b])})
    return shared, per


def run_cfg(cfg, inp):
    shared, per = prep_inputs(cfg, inp)
    nb = len(per)
    nc = build_nc(cfg)
    in_maps = [dict(shared, **per[b]) for b in range(nb)]
    res = run_bass_kernel_spmd(nc, in_maps, core_ids=list(range(nb)))
    out = np.stack([np.ascontiguousarray(res.results[b]["outT"].T) for b in range(nb)], axis=0)
    return out.astype(np.float32)


def kernel(**inputs):
    cfg = Cfg()
    return run_cfg(cfg, inputs)
```

```python
import numpy as np
import concourse.bass as bass
import concourse.mybir as mybir
from concourse.bass_utils import run_bass_kernel_spmd

F32 = mybir.dt.float32
BF16 = mybir.dt.bfloat16
AF = mybir.ActivationFunctionType
ALU = mybir.AluOpType

RMS_EPS = 1e-6
LN_EPS = 1e-5
WINS = (2, 4, 8, 16)
CK = 31


class Cfg:
    def __init__(self, D=4096, S=2048, HID=11008, L=2, T=256):
        self.D, self.S, self.HID, self.L, self.T = D, S, HID, L, T
        self.DC = D // 128
        self.W = D // 4
        self.WC = self.W // 128
        self.GD = self.W // 4
        self.GC = self.GD // 128
        self.HC = HID // 128
        self.NIN = 8 * self.W + 4 * D
        self.NT = S // T
        self.HEADS = self.W // 128
        assert self.GD % 128 == 0 and HID % 256 == 0 and S % T == 0 and T % 128 == 0
        o = 0
        self.po = {}
        for name, n in (("nmg", self.DC), ("adab", 6 * self.DC), ("pscale", self.WC),
                        ("sconv", 3 * self.WC), ("cdw", CK * self.WC), ("cdb", self.WC),
                        ("clg", self.WC), ("clb", self.WC), ("gateb", 4 * self.DC),
                        ("nfg", self.DC), ("fconv", 3 * 2 * self.HC)):
            self.po[name] = o
            o += n
        self.NPL = o
        self.NP = o * L + self.DC
        self.KG = []
        k = 0
        while k < self.HC:
            n = min(32, self.HC - k)
            self.KG.append((k, n))
            k += n
        self.NBC = 3 * self.W


class Buf:
    __slots__ = ("w", "r")

    def __init__(self):
        self.w = None
        self.r = {}


class Eng:
    def __init__(self, name, sem, is_pe=False):
        self.name, self.sem, self.is_pe = name, sem, is_pe
        self.count = 0
        self.known = {}
        self.ops = []


class Prog:
    NDS = 24

    def __init__(self, nc, sems, dsems):
        self.nc = nc
        self.pe = Eng("pe", sems[0], True)
        self.act = Eng("act", sems[1])
        self.dve = Eng("dve", sems[2])
        self.pool = Eng("pool", sems[3])
        self.sp = Eng("sp", None)
        self.dsems = dsems
        self.dvals = [0] * len(dsems)
        self.dma_i = 0
        self.out_toks = []

    def _deps(self, e, reads, writes, extra=()):
        deps = {}

        def add(tok):
            if tok is None:
                return
            k = id(tok[0])
            if k not in deps or deps[k][1] < tok[1]:
                deps[k] = tok
        for b in reads:
            add(b.w)
        for b in writes:
            add(b.w)
            for t in b.r.values():
                add(t)
        for t in extra:
            add(t)
        for k, (sem, val) in deps.items():
            if e.is_pe and sem is e.sem:
                continue
            if e.known.get(k, 0) >= val:
                continue
            e.known[k] = val
            e.ops.append(("wait", sem, val))

    def _mark(self, tok, reads, writes):
        for b in writes:
            b.w = tok
            b.r = {}
        k = id(tok[0])
        for b in reads:
            if b.w is not tok:
                b.r[k] = tok

    def op(self, e, fn, reads=(), writes=()):
        self._deps(e, reads, writes)
        e.count += 1
        tok = (e.sem, e.count)
        e.ops.append(("op", fn, True))
        self._mark(tok, reads, writes)

    def group(self, e, fns, reads=(), writes=()):
        self._deps(e, reads, writes)
        e.count += 1
        tok = (e.sem, e.count)
        n = len(fns)
        for i, fn in enumerate(fns):
            e.ops.append(("op", fn, i == n - 1))
        self._mark(tok, reads, writes)

    def dma(self, q, out_ap, in_ap, reads=(), writes=(), is_out=False):
        i = self.dma_i % len(self.dsems)
        self.dma_i += 1
        s = self.dsems[i]
        prev = self.dvals[i]
        self.dvals[i] += 16
        val = self.dvals[i]
        extra = [(s, prev)] if prev > 0 else []
        self._deps(q, reads, writes, extra)
        q.ops.append(("dma", out_ap, in_ap, s))
        tok = (s, val)
        self._mark(tok, reads, writes)
        if is_out:
            self.out_toks.append(tok)

    def emit(self, e, h):
        for o in e.ops:
            if o[0] == "wait":
                h.wait_ge(o[1], o[2])
            elif o[0] == "op":
                ins = o[1](h)
                if o[2]:
                    ins.then_inc(e.sem, 1)
            else:
                h.dma_start(out=o[1], in_=o[2]).then_inc(o[3], 16)


def build_nc(cfg):
    D, S, HID, L, T = cfg.D, cfg.S, cfg.HID, cfg.L, cfg.T
    DC, W, WC, GD, GC, HC, NIN, NT, HEADS = (cfg.DC, cfg.W, cfg.WC, cfg.GD, cfg.GC, cfg.HC,
                                             cfg.NIN, cfg.NT, cfg.HEADS)
    nc = bass.Bass("TRN2", target_bir_lowering=False)

    def din(name, shape, dt=F32):
        return nc.dram_tensor(name, list(shape), dt, kind="ExternalInput").ap()

    xT = din("xT", [D, S])
    cT = din("cT", [128, DC])
    ada_w = din("ada_w", [L * D, 6 * D])
    w_in = din("w_in", [L * D, NIN])
    w_branch = din("w_branch", [L * D, D])
    w_out = din("w_out", [L * D, D])
    ffn_up = din("ffn_up", [L * D, 2 * HID])
    ffn_down = din("ffn_down", [L * HID, D])
    pool_w = din("pool_w", [L * 4 * GD, GD])
    params_d = din("params", [128, cfg.NP])
    bcast_d = din("bcast", [L * 128, cfg.NBC])
    consts_d = din("consts", [128, 64 + 128 + 128])
    sguT = din("sguT", [L * HEADS * 128, 128])
    outT = nc.dram_tensor("outT", [D, S], F32, kind="ExternalOutput").ap()

    def dint(name, n, kc):
        return nc.dram_tensor(name, [n, 128, kc * 128], BF16, kind="Internal").ap()

    class Slabs:
        def __init__(self, name, n, kc, ngroups):
            self.ap = dint(name, n, kc)
            self.kc = kc
            self.bufs = [[Buf() for _ in range(ngroups)] for _ in range(n)]
            self.done = [False] * n
            self.src = None

    def ng(kc):
        return (kc + 7) // 8

    S_in = [Slabs(f"s_in{l}", NIN // 128, DC, ng(DC)) for l in range(L)]
    S_br = [Slabs(f"s_br{l}", 4 * DC, WC, ng(WC)) for l in range(L)]
    S_out = [Slabs(f"s_out{l}", DC, DC, ng(DC)) for l in range(L)]
    S_up = [Slabs(f"s_up{l}", 2 * HC, DC, ng(DC)) for l in range(L)]
    S_dn = [[Slabs(f"s_dn{l}_{g}", DC, n, ng(n)) for g, (k0, n) in enumerate(cfg.KG)]
            for l in range(L)]
    S_pw = [Slabs(f"s_pw{l}", 4 * GC, GC, ng(GC)) for l in range(L)]

    for l in range(L):
        S_in[l].src = lambda idx, l=l: w_in[l * D:(l + 1) * D, idx * 128:(idx + 1) * 128]
        S_br[l].src = lambda idx, l=l: w_branch[l * D + (idx // DC) * W: l * D + (idx // DC + 1) * W,
                                               (idx % DC) * 128:(idx % DC + 1) * 128]
        S_out[l].src = lambda idx, l=l: w_out[l * D:(l + 1) * D, idx * 128:(idx + 1) * 128]
        S_up[l].src = lambda idx, l=l: ffn_up[l * D:(l + 1) * D, idx * 128:(idx + 1) * 128]
        for g, (k0, n) in enumerate(cfg.KG):
            S_dn[l][g].src = lambda idx, l=l, k0=k0, n=n: ffn_down[l * HID + k0 * 128: l * HID + (k0 + n) * 128,
                                                                  idx * 128:(idx + 1) * 128]
        S_pw[l].src = lambda idx, l=l: pool_w[(l * 4 + idx // GC) * GD:(l * 4 + idx // GC + 1) * GD,
                                              (idx % GC) * 128:(idx % GC + 1) * 128]

    from contextlib import ExitStack
    es = ExitStack()

    def sb(name, shape, dt):
        return es.enter_context(nc.sbuf_tensor(name, list(shape), dt))

    TW = T + 32
    NTMP = 12
    NSLOT = 4
    x_sb = sb("x_sb", [128, DC * T], F32)
    h_sb = sb("h_sb", [128, DC * T], BF16)
    RAU = max(HC, 4 * WC + DC, 80)
    ra32 = sb("ra", [128, RAU * T // 2], F32)
    ra16 = ra32[:, :].bitcast(BF16)
    slots = [sb(f"slot{i}", [128, 32 * 128], BF16) for i in range(NSLOT)]
    fst = [sb(f"fst{i}", [128, 1024], F32) for i in range(4)]
    scr = sb("scr", [128, max(2 * W, WC * T, 2048)], F32)
    NSCR = max(2 * W, WC * T, 2048) // T
    vT = sb("vT", [128, 2 * W], BF16)
    bc = sb("bc", [128, cfg.NBC], F32)
    cst = sb("cst", [128, 64 + 128 + 128], F32)
    par = sb("par", [128, cfg.NP], F32)
    modd = sb("modd", [128, L * 6 * DC], F32)
    tmps = [sb(f"tmp{i}", [128, TW], F32) for i in range(NTMP)]
    pl = [sb(f"pl{i}", [128, T], BF16) for i in range(2 * GC)]
    wm = sb("wm", [128, L * HEADS * 128], BF16)
    ones32 = sb("ones32", [128, 128], F32)
    condT = sb("condT", [128, DC], F32)
    rowsb = [sb(f"rowsb{i}", [1, 256], F32) for i in range(2)]
    smalls = sb("smalls", [128, 64], F32)
    rstd_t = sb("rstd_t", [128, T], F32)
    carA = sb("carA", [128, L * WC * 16], F32)
    carC = sb("carC", [128, L * WC * 2], F32)
    carD = sb("carD", [128, L * WC * 30], F32)
    carF = sb("carF", [128, L * 2 * HC * 2], F32)
    pst = [es.enter_context(nc.psum_tensor(f"ps{i}", [128, 512], F32)) for i in range(8)]

    sems = [es.enter_context(nc.semaphore(f"se{i}")) for i in range(4)]
    dsems = [es.enter_context(nc.semaphore(f"sd{i}")) for i in range(Prog.NDS)]
    P = Prog(nc, sems, dsems)
    PE, ACT, DVE, POOL, SP = P.pe, P.act, P.dve, P.pool, P.sp

    xb = [Buf() for _ in range(DC)]
    hb = [Buf() for _ in range(DC)]
    rab = [Buf() for _ in range(RAU)]
    slotb = [[Buf() for _ in range(4)] for _ in range(NSLOT)]
    fstb = [Buf() for _ in range(4)]
    scrb = [Buf() for _ in range(NSCR)]
    vTb = [Buf() for _ in range(2)]
    bcb, cstb, parb, wmb, onesb, condb, smallb = Buf(), Buf(), Buf(), Buf(), Buf(), Buf(), Buf()
    modb = [Buf() for _ in range(L)]
    tmpb = [Buf() for _ in range(NTMP)]
    plb = [Buf() for _ in range(2 * GC)]
    rowb = [Buf(), Buf()]
    carAb, carCb, carDb, carFb = Buf(), Buf(), Buf(), Buf()
    rstdb = Buf()
    psb = [Buf() for _ in range(16)]
    st = {"ps": 0, "tmp": 0, "slot": 0, "cast": 0}

    def psum(reserved=None):
        if reserved is None:
            i = st["ps"] % 7
            st["ps"] += 1
        else:
            i = 7
        return pst[i][:, 0:256], psb[i]

    def tmp():
        i = st["tmp"] % NTMP
        st["tmp"] += 1
        return tmps[i], tmpb[i]

    def xs(kc):
        return x_sb[:, kc * T:(kc + 1) * T]

    def hs(kc):
        return h_sb[:, kc * T:(kc + 1) * T]

    def ra(u):
        return ra16[:, u * T:(u + 1) * T]

    def pcol(l, name, j):
        o = l * cfg.NPL + cfg.po[name] + j
        return par[:, o:o + 1]

    def mcol(l, k, j):
        o = (l * 6 + k) * DC + j
        return modd[:, o:o + 1]

    NST32, NST16 = 4, 4
    st32 = [ra32[:, i * 2048:(i + 1) * 2048] for i in range(4)]
    st32b = [rab[i * 16:(i + 1) * 16] for i in range(4)]
    scr16 = scr[:, :].bitcast(BF16)
    st16 = [ra16[:, 16384 + i * 2048: 16384 + (i + 1) * 2048] for i in range(2)] + \
           [scr16[:, i * 2048:(i + 1) * 2048] for i in range(2)]
    u16 = 4096 // (T * 4)
    st16b = [rab[64 + i * 8: 64 + (i + 1) * 8] for i in range(2)] + [scrb[i * u16:(i + 1) * u16] for i in range(2)]
    assert RAU >= 80 and NSCR * T * 4 >= 8192

    P.dma(ACT, par[:, :], params_d, writes=[parb])
    P.dma(ACT, cst[:, :], consts_d, writes=[cstb])
    P.dma(ACT, condT[:, :], cT, writes=[condb])
    P.op(ACT, lambda e: e.activation(out=condT[:, :], in_=condT[:, :], func=AF.Silu), writes=[condb])
    P.op(DVE, lambda e: e.tensor_copy(out=ones32[:, :], in_=cst[:, 192:320]), reads=[cstb], writes=[onesb])
    for cb, ct in ((carAb, carA), (carCb, carC), (carDb, carD), (carFb, carF)):
        P.op(POOL, lambda e, ct=ct: e.memset(ct[:, :], 0.0), writes=[cb])

    stc = {"i": 0, "o": 0, "f": 0}

    def stage_in(src_ap, nk, ncol):
        i = stc["i"] % NST32
        stc["i"] += 1
        v = st32[i].rearrange("p (k c) -> p k c", c=256)[:, 0:nk, 0:ncol]
        P.dma(SP, v, src_ap.rearrange("(k p) c -> p k c", p=128), writes=st32b[i])
        return i, v

    cast_engs = [DVE, POOL, ACT]

    def cast(out_ap, in_ap, reads, writes):
        e = cast_engs[st["cast"] % 3]
        st["cast"] += 1
        if e is ACT:
            P.op(e, lambda g: g.activation(out=out_ap, in_=in_ap, func=AF.Copy), reads=reads, writes=writes)
        else:
            P.op(e, lambda g: g.tensor_copy(out=out_ap, in_=in_ap), reads=reads, writes=writes)

    pending = []
    LAG = 2

    def flush_stores(n_keep):
        while len(pending) > n_keep:
            a = pending.pop(0)
            P.dma(SP, a[0], a[1], reads=a[2], writes=a[3])

    def precast(src, K, N, slabs, slab_base):
        KC = K // 128
        MC = N // 128
        m = 0
        oi = 0
        while m < MC:
            nm = 2 if m + 1 < MC else 1
            for g in range(ng(KC)):
                k0 = g * 8
                nk = min(8, KC - k0)
                i, v = stage_in(src[k0 * 128:(k0 + nk) * 128, m * 128:(m + nm) * 128], nk, nm * 128)
                j = stc["o"] % NST16
                stc["o"] += 1
                o16 = st16[j].rearrange("p (m k c) -> p m k c", m=2, c=128)
                for mm in range(nm):
                    cast(o16[:, mm, 0:nk, :], v[:, :, mm * 128:(mm + 1) * 128], st32b[i], st16b[j])
                dst = slabs.ap[slab_base + m:slab_base + m + nm, :, k0 * 128:(k0 + nk) * 128]
                pending.append((dst.rearrange("m p (k c) -> p m k c", c=128), o16[:, 0:nm, 0:nk, :],
                                st16b[j], [slabs.bufs[slab_base + m + mm][g] for mm in range(nm)]))
                flush_stores(LAG)
            m += nm

    def ada_layer(l):
        pT, pTb = psum(reserved=0)
        import os
        KA = int(os.environ.get("KADA", "9"))
        for s in range(6 * D // 256):
            prow, prb = psum()
            for g in range(ng(DC)):
                k0 = g * 8
                nk = min(8, DC - k0)
                i, v = stage_in(ada_w[l * D + k0 * 128: l * D + (k0 + nk) * 128, s * 256:(s + 1) * 256], nk, 256)
                fns = [(lambda e, kk=kk, v=v, prow=prow, k0=k0:
                        e.matmul(prow[0:1, :], lhsT=condT[:, k0 + kk:k0 + kk + 1], rhs=v[:, kk, :],
                                 start=(k0 + kk == 0), stop=(k0 + kk == DC - 1))) for kk in range(nk)]
                if KA >= 2:
                    P.group(PE, fns, reads=st32b[i] + [condb], writes=[prb])
            r = s % 2
            if KA < 3:
                continue
            P.op(ACT, lambda e, r=r, prow=prow: e.activation(out=rowsb[r][0:1, :], in_=prow[0:1, :], func=AF.Copy),
                 reads=[prb], writes=[rowb[r]])
            fns = [(lambda e, mm=mm, r=r, s=s:
                    e.matmul(pT[:, 2 * s + mm:2 * s + mm + 1], lhsT=rowsb[r][0:1, mm * 128:(mm + 1) * 128],
                             rhs=ones32[0:1, 0:1], start=True, stop=True)) for mm in range(2)]
            if KA >= 4:
                P.group(PE, fns, reads=[rowb[r], onesb], writes=[pTb])
        if KA < 5:
            return
        o = l * cfg.NPL + cfg.po["adab"]
        raw, rawb = tmp()
        P.op(DVE, lambda e: e.tensor_tensor(out=raw[:, 0:6 * DC], in0=pT[:, 0:6 * DC], in1=par[:, o:o + 6 * DC], op=ALU.add),
             reads=[pTb, parb], writes=[rawb])
        base = l * 6 * DC
        og = l * cfg.NPL + cfg.po["nmg"]
        of = l * cfg.NPL + cfg.po["nfg"]
        P.op(DVE, lambda e: e.scalar_tensor_tensor(out=modd[:, base:base + DC], in0=raw[:, DC:2 * DC], scalar=1.0,
                                                   in1=par[:, og:og + DC], op0=ALU.add, op1=ALU.mult),
             reads=[rawb, parb], writes=[modb[l]])
        P.op(DVE, lambda e: e.tensor_copy(out=modd[:, base + DC:base + 2 * DC], in_=raw[:, 0:DC]), reads=[rawb], writes=[modb[l]])
        P.op(DVE, lambda e: e.tensor_copy(out=modd[:, base + 2 * DC:base + 3 * DC], in_=raw[:, 2 * DC:3 * DC]), reads=[rawb], writes=[modb[l]])
        P.op(DVE, lambda e: e.scalar_tensor_tensor(out=modd[:, base + 3 * DC:base + 4 * DC], in0=raw[:, 4 * DC:5 * DC], scalar=1.0,
                                                   in1=par[:, of:of + DC], op0=ALU.add, op1=ALU.mult),
             reads=[rawb, parb], writes=[modb[l]])
        P.op(DVE, lambda e: e.tensor_copy(out=modd[:, base + 4 * DC:base + 5 * DC], in_=raw[:, 3 * DC:4 * DC]), reads=[rawb], writes=[modb[l]])
        P.op(DVE, lambda e: e.tensor_copy(out=modd[:, base + 5 * DC:base + 6 * DC], in_=raw[:, 5 * DC:6 * DC]), reads=[rawb], writes=[modb[l]])

    import os
    DBG = os.environ.get("KDBG", "")
    for l in range(L):
        if "noada" not in DBG:
            ada_layer(l)
        i, v = stage_in(sguT[l * HEADS * 128:(l + 1) * HEADS * 128, :], HEADS, 128)
        for hh in range(HEADS):
            o = (l * HEADS + hh) * 128
            P.op(DVE, lambda e, v=v, hh=hh, o=o: e.tensor_tensor(out=wm[:, o:o + 128], in0=v[:, hh, :], in1=cst[:, 64:192], op=ALU.mult),
                 reads=st32b[i] + [cstb], writes=[wmb])
    flush_stores(0)

    def load_slab(slabs, idx):
        i = st["slot"] % NSLOT
        st["slot"] += 1
        n = slabs.kc * 128
        sl = slots[i]
        if slabs.done[idx]:
            P.dma(SP, sl[:, 0:n], slabs.ap[idx], reads=slabs.bufs[idx], writes=slotb[i])
            return sl, slotb[i]
        for g in range(ng(slabs.kc)):
            k0 = g * 8
            nk = min(8, slabs.kc - k0)
            j = stc["f"] % 4
            stc["f"] += 1
            v = fst[j][:, :].rearrange("p (k c) -> p k c", c=128)[:, 0:nk, :]
            P.dma(SP, v, slabs.src(idx)[k0 * 128:(k0 + nk) * 128, :].rearrange("(k p) c -> p k c", p=128), writes=[fstb[j]])
            o_ap = sl[:, k0 * 128:(k0 + nk) * 128].rearrange("p (k c) -> p k c", c=128)
            if stc["f"] % 2 == 0:
                P.op(DVE, lambda e, o_ap=o_ap, v=v: e.tensor_copy(out=o_ap, in_=v), reads=[fstb[j]], writes=[slotb[i][g]])
            else:
                P.op(ACT, lambda e, o_ap=o_ap, v=v: e.activation(out=o_ap, in_=v, func=AF.Copy), reads=[fstb[j]], writes=[slotb[i][g]])
        pending.append((slabs.ap[idx], sl[:, 0:n], slotb[i][0:ng(slabs.kc)], slabs.bufs[idx]))
        flush_stores(LAG)
        slabs.done[idx] = True
        return sl, slotb[i]

    def proj(slabs, idx, rhs_fn, rhs_bufs, nk=None):
        sl, slb = load_slab(slabs, idx)
        nk = slabs.kc
        ps, pb = psum()
        fns = [(lambda e, k=k: e.matmul(ps, lhsT=sl[:, k * 128:(k + 1) * 128], rhs=rhs_fn(k),
                                        start=(k == 0), stop=(k == nk - 1))) for k in range(nk)]
        P.group(PE, fns, reads=slb + rhs_bufs, writes=[pb])
        return ps, pb

    def rsqrt_inplace(ap, buf):
        P.op(ACT, lambda e: e.activation(out=ap, in_=ap, func=AF.Sqrt), writes=[buf])
        P.op(DVE, lambda e: e.reciprocal(out=ap, in_=ap), writes=[buf])

    def rms_to_h(Acol, Bcol):
        ps, pb = psum()
        for kc in range(DC):
            sq, sqb = tmp()
            P.op(ACT, lambda e, kc=kc, sq=sq: e.activation(out=sq[:, 0:T], in_=xs(kc), func=AF.Square),
                 reads=[xb[kc]], writes=[sqb])
            P.group(PE, [lambda e, kc=kc, sq=sq: e.matmul(ps, lhsT=ones32[:, :], rhs=sq[:, 0:T],
                                                         start=(kc == 0), stop=(kc == DC - 1))],
                    reads=[sqb, onesb], writes=[pb])
        rstd, rsb = rstd_t, rstdb
        P.op(DVE, lambda e: e.tensor_scalar(out=rstd[:, 0:T], in0=ps, scalar1=1.0 / D, scalar2=RMS_EPS,
                                            op0=ALU.mult, op1=ALU.add), reads=[pb], writes=[rsb])
        rsqrt_inplace(rstd[:, 0:T], rsb)
        return rstd, rsb

    def norm_mod(l, ka, kb):
        rstd, rsb = rms_to_h(None, None)
        for kc in range(DC):
            t, tb = tmp()
            P.op(DVE, lambda e, kc=kc, t=t: e.tensor_tensor(out=t[:, 0:T], in0=xs(kc), in1=rstd[:, 0:T], op=ALU.mult),
                 reads=[xb[kc], rsb], writes=[tb])
            P.op(ACT, lambda e, kc=kc, t=t: e.activation(out=hs(kc), in_=t[:, 0:T], func=AF.Identity,
                                                         bias=mcol(l, kb, kc), scale=mcol(l, ka, kc)),
                 reads=[tb, modb[l]], writes=[hb[kc]])

    hfn = lambda k: hs(k)

    KS = int(os.environ.get("KSTG", "99"))

    def layer(l, ti):
        first = (ti == 0)
        P.dma(ACT, bc[:, :], bcast_d[l * 128:(l + 1) * 128, :], writes=[bcb])
        norm_mod(l, 0, 1)
        YA, YB, YC, YD = 0, WC, 2 * WC, 3 * WC
        YDU = 4 * WC
        MG = 4 * WC
        if KS < 4:
            return
        yu = 1
        for c in range(WC):
            ps_a, pab = proj(S_in[l], 6 * WC + c, hfn, hb)
            ps_g, pgb = proj(S_in[l], 7 * WC + c, hfn, hb)
            sg, sgb = tmp()
            P.op(ACT, lambda e, sg=sg, ps_g=ps_g: e.activation(out=sg[:, 0:T], in_=ps_g, func=AF.Sigmoid), reads=[pgb], writes=[sgb])
            yb, ybb = tmp()
            co = (l * WC + c) * 30
            P.op(POOL, lambda e, yb=yb, co=co: e.tensor_copy(out=yb[:, 0:30], in_=carD[:, co:co + 30]), reads=[carDb], writes=[ybb])
            P.op(DVE, lambda e, yb=yb, sg=sg, ps_a=ps_a: e.tensor_tensor(out=yb[:, 30:30 + T], in0=sg[:, 0:T], in1=ps_a, op=ALU.mult),
                 reads=[sgb, pab], writes=[ybb])
            P.op(POOL, lambda e, yb=yb, co=co: e.tensor_copy(out=carD[:, co:co + 30], in_=yb[:, T:T + 30]), reads=[ybb], writes=[carDb])
            yd = ra32[:, (YDU + 2 * c) * (T // 2):(YDU + 2 * c) * (T // 2) + T]
            P.op(DVE, lambda e, yd=yd, yb=yb, c=c: e.tensor_scalar(out=yd, in0=yb[:, 0:T], scalar1=pcol(l, "cdw", c),
                                                                 scalar2=pcol(l, "cdb", c), op0=ALU.mult, op1=ALU.add),
                 reads=[ybb, parb], writes=rab[YDU + 2 * c:YDU + 2 * c + 2])
            for k in range(1, CK):
                P.op(DVE, lambda e, yd=yd, yb=yb, c=c, k=k: e.scalar_tensor_tensor(
                    out=yd, in0=yb[:, k:k + T], scalar=pcol(l, "cdw", k * WC + c), in1=yd, op0=ALU.mult, op1=ALU.add),
                    reads=[ybb, parb], writes=rab[YDU + 2 * c:YDU + 2 * c + 2])
        ps_s, pssb = psum()
        ps_q, psqb = psum()
        for c in range(WC):
            yd = ra32[:, (YDU + 2 * c) * (T // 2):(YDU + 2 * c) * (T // 2) + T]
            P.group(PE, [lambda e, yd=yd, c=c: e.matmul(ps_s, lhsT=ones32[:, :], rhs=yd, start=(c == 0), stop=(c == WC - 1))],
                    reads=rab[YDU + 2 * c:YDU + 2 * c + 2] + [onesb], writes=[pssb])
            sq, sqb = tmp()
            P.op(ACT, lambda e, sq=sq, yd=yd: e.activation(out=sq[:, 0:T], in_=yd, func=AF.Square), reads=rab[YDU + 2 * c:YDU + 2 * c + 2], writes=[sqb])
            P.group(PE, [lambda e, sq=sq, c=c: e.matmul(ps_q, lhsT=ones32[:, :], rhs=sq[:, 0:T], start=(c == 0), stop=(c == WC - 1))],
                    reads=[sqb, onesb], writes=[psqb])
        mu, mub = tmp()
        rs, rsb = tmp()
        P.op(DVE, lambda e: e.tensor_scalar(out=mu[:, 0:T], in0=ps_s, scalar1=1.0 / W, scalar2=None, op0=ALU.mult), reads=[pssb], writes=[mub])
        P.op(DVE, lambda e: e.tensor_tensor(out=rs[:, 0:T], in0=mu[:, 0:T], in1=mu[:, 0:T], op=ALU.mult), reads=[mub], writes=[rsb])
        P.op(DVE, lambda e: e.scalar_tensor_tensor(out=rs[:, 0:T], in0=ps_q, scalar=1.0 / W, in1=rs[:, 0:T], op0=ALU.mult, op1=ALU.subtract),
             reads=[psqb], writes=[rsb])
        P.op(DVE, lambda e: e.tensor_scalar(out=rs[:, 0:T], in0=rs[:, 0:T], scalar1=LN_EPS, scalar2=None, op0=ALU.add), writes=[rsb])
        rsqrt_inplace(rs[:, 0:T], rsb)
        for c in range(WC):
            yd = ra32[:, (YDU + 2 * c) * (T // 2):(YDU + 2 * c) * (T // 2) + T]
            P.op(DVE, lambda e, yd=yd: e.tensor_tensor(out=yd, in0=yd, in1=mu[:, 0:T], op=ALU.subtract), reads=[mub], writes=rab[YDU + 2 * c:YDU + 2 * c + 2])
            P.op(DVE, lambda e, yd=yd: e.tensor_tensor(out=yd, in0=yd, in1=rs[:, 0:T], op=ALU.mult), reads=[rsb], writes=rab[YDU + 2 * c:YDU + 2 * c + 2])
            P.op(ACT, lambda e, yd=yd, c=c: e.activation(out=ra(YD + c), in_=yd, func=AF.Silu, bias=pcol(l, "clb", c), scale=pcol(l, "clg", c)),
                 reads=rab[YDU + 2 * c:YDU + 2 * c + 2] + [parb], writes=[rab[YD + c]])
        if KS < 1:
            return
        for g in range(4):
            w = WINS[g]
            for c2 in range(GC):
                c = g * GC + c2
                ps, pb = proj(S_in[l], c, hfn, hb)
                KB = int(os.environ.get("KA2", "9"))
                if KB < 2:
                    continue
                ab, abb = tmp()
                co = (l * WC + c) * 16
                P.op(POOL, lambda e, ab=ab, co=co: e.tensor_copy(out=ab[:, 0:16], in_=carA[:, co:co + 16]), reads=[carAb], writes=[abb])
                P.op(ACT, lambda e, ab=ab, ps=ps: e.activation(out=ab[:, 16:16 + T], in_=ps, func=AF.Copy), reads=[pb], writes=[abb])
                P.op(POOL, lambda e, ab=ab, co=co: e.tensor_copy(out=carA[:, co:co + 16], in_=ab[:, T:T + 16]), reads=[abb], writes=[carAb])
                if KB < 3:
                    continue
                WB = 16 + T
                cur, curb = ab, abb
                lag = 1
                while lag < w:
                    nx, nxb = tmp()
                    lo = 2 * lag - 1
                    P.op(DVE, lambda e, nx=nx, cur=cur, lag=lag, lo=lo: e.tensor_tensor(
                        out=nx[:, lo:WB], in0=cur[:, lo:WB], in1=cur[:, lo - lag:WB - lag], op=ALU.add),
                        reads=[curb], writes=[nxb])
                    cur, curb = nx, nxb
                    lag *= 2
                pi = c2
                if KB < 4:
                    continue
                P.op(DVE, lambda e, cur=cur, ab=ab, pi=pi, w=w: e.scalar_tensor_tensor(
                    out=pl[pi][:, :], in0=cur[:, 16:16 + T], scalar=1.0 / w, in1=ab[:, 16:16 + T],
                    op0=ALU.mult, op1=ALU.subtract), reads=[curb, abb], writes=[plb[pi]])
                if first and KB >= 5:
                    t16, t16b = tmp()
                    P.op(DVE, lambda e, t16=t16, cur=cur, g=g: e.tensor_tensor(
                        out=t16[:, 0:16], in0=cur[:, 16:32], in1=cst[:, g * 16:(g + 1) * 16], op=ALU.mult),
                        reads=[curb, cstb], writes=[t16b])
                    P.op(DVE, lambda e, t16=t16, ab=ab, pi=pi: e.tensor_tensor(
                        out=pl[pi][:, 0:16], in0=t16[:, 0:16], in1=ab[:, 16:32], op=ALU.subtract),
                        reads=[t16b, abb], writes=[plb[pi]])
            for m2 in range(GC):
                if int(os.environ.get("KA2", "9")) < 6:
                    continue
                c = g * GC + m2
                ps, pb = proj(S_pw[l], g * GC + m2, lambda k: pl[k][:, :], plb[0:GC])
                P.op(ACT, lambda e, ps=ps, c=c: e.activation(out=ra(YA + c), in_=ps, func=AF.Copy, scale=pcol(l, "pscale", c)),
                     reads=[pb, parb], writes=[rab[YA + c]])
        if KS < 2:
            return
        for c in range(WC):
            ps, pb = proj(S_in[l], WC + c, hfn, hb)
            P.op(ACT, lambda e, ps=ps, c=c: e.activation(out=ra(YB + c), in_=ps, func=AF.Gelu), reads=[pb], writes=[rab[YB + c]])
        NS2 = T // 128
        su = W // T if W >= T else 1
        for j in range(WC):
            sl, slb = load_slab(S_in[l], 2 * WC + j)
            ps, pb = psum()
            fns = []
            for s2 in range(NS2):
                for k in range(DC):
                    fns.append(lambda e, s2=s2, k=k, sl=sl, ps=ps: e.matmul(
                        ps[:, s2 * 128:(s2 + 1) * 128], lhsT=h_sb[:, k * T + s2 * 128:k * T + (s2 + 1) * 128],
                        rhs=sl[:, k * 128:(k + 1) * 128], start=(k == 0), stop=(k == DC - 1)))
            P.group(PE, fns, reads=slb + hb, writes=[pb])
            for s2 in range(NS2):
                P.op(ACT, lambda e, s2=s2, j=j, ps=ps: e.activation(
                    out=scr[:, s2 * W + j * 128:s2 * W + (j + 1) * 128], in_=ps[:, s2 * 128:(s2 + 1) * 128], func=AF.Gelu),
                    reads=[pb], writes=scrb[s2 * su:(s2 + 1) * su])
        for s2 in range(NS2):
            vb = scr[:, s2 * W:(s2 + 1) * W]
            vbb = scrb[s2 * su:(s2 + 1) * su]
            nst = (W + 511) // 512
            so = 0
            for q in range(nst):
                a0, a1 = q * 512, min(W, (q + 1) * 512)
                P.op(DVE, lambda e, q=q, a0=a0, a1=a1, vb=vb: e.bn_stats(out=smalls[:, q * 6:(q + 1) * 6], in_=vb[:, a0:a1]),
                     reads=vbb, writes=[smallb])
            P.op(DVE, lambda e: e.bn_aggr(out=smalls[:, 32:34], in_=smalls[:, 0:nst * 6]), writes=[smallb])
            P.op(DVE, lambda e: e.tensor_scalar(out=smalls[:, 34:35], in0=smalls[:, 33:34], scalar1=LN_EPS, scalar2=None,
                                                op0=ALU.add), writes=[smallb])
            rsqrt_inplace(smalls[:, 34:35], smallb)
            P.op(DVE, lambda e, vb=vb: e.tensor_scalar(out=vb, in0=vb, scalar1=smalls[:, 32:33], scalar2=smalls[:, 34:35],
                                                       op0=ALU.subtract, op1=ALU.mult), reads=[smallb], writes=vbb)
            P.op(DVE, lambda e, vb=vb: e.tensor_tensor(out=vb, in0=vb, in1=bc[:, 0:W], op=ALU.mult), reads=[bcb], writes=vbb)
            P.op(DVE, lambda e, vb=vb, s2=s2: e.tensor_tensor(out=vT[:, s2 * W:(s2 + 1) * W], in0=vb, in1=bc[:, W:2 * W], op=ALU.add),
                 reads=[bcb] + vbb, writes=[vTb[s2]])
        for hh in range(HEADS):
            ps, pb = psum()
            o = (l * HEADS + hh) * 128
            fns = [lambda e, s2=s2, hh=hh, ps=ps, o=o: e.matmul(
                ps[:, s2 * 128:(s2 + 1) * 128], lhsT=vT[:, s2 * W + hh * 128:s2 * W + (hh + 1) * 128],
                rhs=wm[:, o:o + 128], start=True, stop=True) for s2 in range(NS2)]
            P.group(PE, fns, reads=[vTb[0], vTb[1], wmb], writes=[pb])
            t, tb = tmp()
            for s2 in range(NS2):
                P.op(DVE, lambda e, s2=s2, hh=hh, ps=ps, t=t: e.tensor_tensor(
                    out=t[:, s2 * 128:(s2 + 1) * 128], in0=ps[:, s2 * 128:(s2 + 1) * 128],
                    in1=bc[:, 2 * W + hh * 128:2 * W + (hh + 1) * 128], op=ALU.add), reads=[pb, bcb], writes=[tb])
            P.op(DVE, lambda e, hh=hh, t=t: e.tensor_tensor(out=ra(YB + hh), in0=t[:, 0:T], in1=ra(YB + hh), op=ALU.mult),
                 reads=[tb], writes=[rab[YB + hh]])
        if KS < 3:
            return
        for c in range(WC):
            ps_b, pbb = proj(S_in[l], 3 * WC + c, hfn, hb)
            ps_c, pcb = proj(S_in[l], 4 * WC + c, hfn, hb)
            ps_h, phb = proj(S_in[l], 5 * WC + c, hfn, hb)
            cg, cgb = tmp()
            P.op(ACT, lambda e, cg=cg, ps_c=ps_c: e.activation(out=cg[:, 0:T], in_=ps_c, func=AF.Copy), reads=[pcb], writes=[cgb])
            pr, prb = tmp()
            co = (l * WC + c) * 2
            P.op(POOL, lambda e, pr=pr, co=co: e.tensor_copy(out=pr[:, 0:2], in_=carC[:, co:co + 2]), reads=[carCb], writes=[prb])
            P.op(DVE, lambda e, pr=pr, cg=cg, ps_h=ps_h: e.tensor_tensor(out=pr[:, 2:2 + T], in0=cg[:, 0:T], in1=ps_h, op=ALU.mult),
                 reads=[cgb, phb], writes=[prb])
            P.op(POOL, lambda e, pr=pr, co=co: e.tensor_copy(out=carC[:, co:co + 2], in_=pr[:, T:T + 2]), reads=[prb], writes=[carCb])
            ac, acb = tmp()
            P.op(DVE, lambda e, ac=ac, pr=pr, c=c: e.tensor_scalar(out=ac[:, 0:T], in0=pr[:, 0:T], scalar1=pcol(l, "sconv", 0 * WC + c),
                                                                 scalar2=None, op0=ALU.mult), reads=[prb, parb], writes=[acb])
            for k in (1, 2):
                P.op(DVE, lambda e, ac=ac, pr=pr, c=c, k=k: e.scalar_tensor_tensor(
                    out=ac[:, 0:T], in0=pr[:, k:k + T], scalar=pcol(l, "sconv", k * WC + c), in1=ac[:, 0:T],
                    op0=ALU.mult, op1=ALU.add), reads=[prb, parb], writes=[acb])
            P.op(DVE, lambda e, ac=ac, ps_b=ps_b, c=c: e.tensor_tensor(out=ra(YC + c), in0=ac[:, 0:T], in1=ps_b, op=ALU.mult),
                 reads=[acb, pbb], writes=[rab[YC + c]])
        if KS < 5:
            return
        for m in range(DC):
            acc, accb = tmp()
            for i in range(4):
                ps_g, pgb = proj(S_in[l], 8 * WC + i * DC + m, hfn, hb)
                ps_b, pbb = proj(S_br[l], i * DC + m, lambda k, i=i: ra(i * WC + k), rab[i * WC:(i + 1) * WC])
                sg, sgb = tmp()
                P.op(ACT, lambda e, sg=sg, ps_g=ps_g, i=i, m=m: e.activation(out=sg[:, 0:T], in_=ps_g, func=AF.Sigmoid,
                                                                             bias=pcol(l, "gateb", i * DC + m)),
                     reads=[pgb, parb], writes=[sgb])
                if i == 0:
                    P.op(DVE, lambda e, acc=acc, sg=sg, ps_b=ps_b: e.tensor_tensor(out=acc[:, 0:T], in0=sg[:, 0:T], in1=ps_b, op=ALU.mult),
                         reads=[sgb, pbb], writes=[accb])
                else:
                    P.op(DVE, lambda e, sg=sg, ps_b=ps_b: e.tensor_tensor(out=sg[:, 0:T], in0=sg[:, 0:T], in1=ps_b, op=ALU.mult),
                         reads=[pbb], writes=[sgb])
                    if i < 3:
                        P.op(POOL, lambda e, acc=acc, sg=sg: e.tensor_tensor(out=acc[:, 0:T], in0=acc[:, 0:T], in1=sg[:, 0:T], op=ALU.add),
                             reads=[sgb], writes=[accb])
                    else:
                        P.op(POOL, lambda e, acc=acc, sg=sg, m=m: e.tensor_tensor(out=ra(MG + m), in0=acc[:, 0:T], in1=sg[:, 0:T], op=ALU.add),
                             reads=[sgb, accb], writes=[rab[MG + m]])
        if KS < 6:
            return
        for m in range(DC):
            ps, pb = proj(S_out[l], m, lambda k: ra(MG + k), rab[MG:MG + DC])
            P.op(DVE, lambda e, ps=ps, m=m: e.scalar_tensor_tensor(out=xs(m), in0=ps, scalar=mcol(l, 2, m), in1=xs(m),
                                                                 op0=ALU.mult, op1=ALU.add), reads=[pb, modb[l]], writes=[xb[m]])
        if KS < 7:
            return
        norm_mod(l, 3, 4)
        for J in range(HC):
            accs = []
            for half in range(2):
                ch = half * HC + J
                ps, pb = proj(S_up[l], ch, hfn, hb)
                zb, zbb = tmp()
                co = (l * 2 * HC + ch) * 2
                P.op(POOL, lambda e, zb=zb, co=co: e.tensor_copy(out=zb[:, 0:2], in_=carF[:, co:co + 2]), reads=[carFb], writes=[zbb])
                P.op(ACT, lambda e, zb=zb, ps=ps: e.activation(out=zb[:, 2:2 + T], in_=ps, func=AF.Copy), reads=[pb], writes=[zbb])
                P.op(POOL, lambda e, zb=zb, co=co: e.tensor_copy(out=carF[:, co:co + 2], in_=zb[:, T:T + 2]), reads=[zbb], writes=[carFb])
                ac, acb = tmp()
                P.op(DVE, lambda e, ac=ac, zb=zb, ch=ch: e.tensor_scalar(out=ac[:, 0:T], in0=zb[:, 0:T], scalar1=pcol(l, "fconv", ch),
                                                                      scalar2=None, op0=ALU.mult), reads=[zbb, parb], writes=[acb])
                for k in (1, 2):
                    P.op(DVE, lambda e, ac=ac, zb=zb, ch=ch, k=k: e.scalar_tensor_tensor(
                        out=ac[:, 0:T], in0=zb[:, k:k + T], scalar=pcol(l, "fconv", k * 2 * HC + ch), in1=ac[:, 0:T],
                        op0=ALU.mult, op1=ALU.add), reads=[zbb, parb], writes=[acb])
                accs.append((ac, acb))
            (ag, agb), (av, avb) = accs
            P.op(ACT, lambda e, ag=ag: e.activation(out=ag[:, 0:T], in_=ag[:, 0:T], func=AF.Silu), writes=[agb])
            P.op(DVE, lambda e, ag=ag, av=av, J=J: e.tensor_tensor(out=ra(J), in0=ag[:, 0:T], in1=av[:, 0:T], op=ALU.mult),
                 reads=[agb, avb], writes=[rab[J]])
        for m in range(DC):
            ps, pb = psum()
            ngp = len(cfg.KG)
            for g, (k0, n) in enumerate(cfg.KG):
                sl, slb = load_slab(S_dn[l][g], m)
                fns = [(lambda e, k=k, sl=sl, k0=k0, g=g, n=n, ps=ps: e.matmul(
                    ps, lhsT=sl[:, k * 128:(k + 1) * 128], rhs=ra(k0 + k),
                    start=(g == 0 and k == 0), stop=(g == ngp - 1 and k == n - 1))) for k in range(n)]
                P.group(PE, fns, reads=slb + rab[k0:k0 + n], writes=[pb])
            P.op(DVE, lambda e, ps=ps, m=m: e.scalar_tensor_tensor(out=xs(m), in0=ps, scalar=mcol(l, 5, m), in1=xs(m),
                                                                 op0=ALU.mult, op1=ALU.add), reads=[pb, modb[l]], writes=[xb[m]])

    for ti in range(NT):
        if "nomain" in DBG:
            break
        t0 = ti * T
        XG = 4 if DC % 4 == 0 else 1
        cg_ = DC // XG
        for q in range(XG):
            P.dma(ACT, x_sb[:, q * cg_ * T:(q + 1) * cg_ * T].rearrange("p (k t) -> p k t", t=T),
                  xT[q * cg_ * 128:(q + 1) * cg_ * 128, t0:t0 + T].rearrange("(k p) t -> p k t", p=128),
                  writes=xb[q * cg_:(q + 1) * cg_])
        for l in range(L):
            layer(l, ti)
        flush_stores(0)
        rstd, rsb = rms_to_h(None, None)
        ofg = L * cfg.NPL
        for kc in range(DC):
            P.op(DVE, lambda e, kc=kc: e.tensor_tensor(out=xs(kc), in0=xs(kc), in1=rstd[:, 0:T], op=ALU.mult),
                 reads=[rsb], writes=[xb[kc]])
            P.op(ACT, lambda e, kc=kc: e.activation(out=xs(kc), in_=xs(kc), func=AF.Copy, scale=par[:, ofg + kc:ofg + kc + 1]),
                 reads=[parb], writes=[xb[kc]])
        for q in range(XG):
            P.dma(ACT, outT[q * cg_ * 128:(q + 1) * cg_ * 128, t0:t0 + T].rearrange("(k p) t -> p k t", p=128),
                  x_sb[:, q * cg_ * T:(q + 1) * cg_ * T].rearrange("p (k t) -> p k t", t=T),
                  reads=xb[q * cg_:(q + 1) * cg_], is_out=True)
    for tok in P.out_toks:
        ACT.ops.append(("wait", tok[0], tok[1]))

    with nc.Block() as block:
        @block.tensor
        def _(h):
            P.emit(PE, h)

        @block.scalar
        def _(h):
            P.emit(ACT, h)

        @block.vector
        def _(h):
            P.emit(DVE, h)

        @block.gpsimd
        def _(h):
            P.emit(POOL, h)

        @block.sync
        def _(h):
            P.emit(SP, h)
    es.close()
    return nc


def _pp(v):
    v = np.asarray(v, np.float32)
    return np.ascontiguousarray(v.reshape(-1, 128).T)


def prep_inputs(cfg, inp):
    D, S, HID, L = cfg.D, cfg.S, cfg.HID, cfg.L
    W, WC, DC, HC = cfg.W, cfg.WC, cfg.DC, cfg.HC
    f = lambda a: np.ascontiguousarray(np.asarray(a, np.float32))
    cols = []
    for l in range(L):
        cols.append(_pp(inp["norm_mix_g"][l]))
        cols.append(_pp(inp["ada_b"][l]))
        cols.append(_pp(inp["pool_scale"][l]))
        cols.append(np.concatenate([_pp(inp["sconv_w"][l][k]) for k in range(3)], axis=1))
        cols.append(np.concatenate([_pp(inp["conf_dw_w"][l][k]) for k in range(CK)], axis=1))
        cols.append(_pp(inp["conf_dw_b"][l]))
        cols.append(_pp(inp["conf_ln_g"][l]))
        cols.append(_pp(inp["conf_ln_b"][l]))
        cols.append(_pp(inp["gate_b"][l]))
        cols.append(_pp(inp["norm_ffn_g"][l]))
        cols.append(np.concatenate([_pp(inp["ffn_conv"][l][k]) for k in range(3)], axis=1))
    cols.append(_pp(inp["final_g"]))
    params = np.ascontiguousarray(np.concatenate(cols, axis=1))
    assert params.shape == (128, cfg.NP), params.shape
    bc = np.empty((L, 128, cfg.NBC), np.float32)
    for l in range(L):
        bc[l, :, 0:W] = np.asarray(inp["sgu_ln_g"][l])[None, :]
        bc[l, :, W:2 * W] = np.asarray(inp["sgu_ln_b"][l])[None, :]
        bc[l, :, 2 * W:3 * W] = np.asarray(inp["sgu_b"][l]).reshape(1, W)
    consts = np.zeros((128, 320), np.float32)
    for g, w in enumerate(WINS):
        consts[:, g * 16:(g + 1) * 16] = (1.0 / np.minimum(np.arange(16) + 1, w))[None, :]
    consts[:, 64:192] = np.triu(np.ones((128, 128), np.float32))
    consts[:, 192:320] = 1.0
    sguT = np.ascontiguousarray(np.transpose(np.asarray(inp["sgu_w"], np.float32), (0, 1, 3, 2))).reshape(L * cfg.HEADS * 128, 128)
    shared = {
        "ada_w": f(inp["ada_w"]).reshape(L * D, 6 * D),
        "w_in": f(inp["w_in"]).reshape(L * D, cfg.NIN),
        "w_branch": f(inp["w_branch"]).reshape(L * D, D),
        "w_out": f(inp["w_out"]).reshape(L * D, D),
        "ffn_up": f(inp["ffn_up"]).reshape(L * D, 2 * HID),
        "ffn_down": f(inp["ffn_down"]).reshape(L * HID, D),
        "pool_w": f(inp["pool_w"]).reshape(L * 4 * cfg.GD, cfg.GD),
        "params": params,
        "bcast": bc.reshape(L * 128, cfg.NBC),
        "consts": consts,
        "sguT": sguT,
    }
    x = np.asarray(inp["x"], np.float32)
    c = np.asarray(inp["c"], np.float32)
    per = []
    for b in range(x.shape[0]):
        per.append({"xT": np.ascontiguousarray(x[b].T), "cT": _pp(c[b])})
    return shared, per


def run_cfg(cfg, inp):
    shared, per = prep_inputs(cfg, inp)
    nb = len(per)
    nc = build_nc(cfg)
    in_maps = [dict(shared, **per[b]) for b in range(nb)]
    res = run_bass_kernel_spmd(nc, in_maps, core_ids=list(range(nb)))
    out = np.stack([np.ascontiguousarray(res.results[b]["outT"].T) for b in range(nb)], axis=0)
    return out.astype(np.float32)


def kernel(**inputs):
    cfg = Cfg()
    return run_cfg(cfg, inputs)
```
